# Optimizing a Trainium2 kernel written in Bass

```python
import math
import jax, jax.numpy as jnp
from jax import lax
import numpy as np

D_MODEL = 1024
BATCH = 8
SEQ = 4096
DEPTH = 1

HEAD_DIM = 64
N_Q_HEADS = 8
N_KV_HEADS = 2
GQ = N_Q_HEADS // N_KV_HEADS
WINDOW = 128
BLOCK = 128
ROPE_THETA = 10000.0
D_A = N_Q_HEADS * HEAD_DIM
D_KV = N_KV_HEADS * HEAD_DIM

CHUNK = 128
N_GROUPS = 4
D_B = D_MODEL // 2
GROUP_W = D_B // N_GROUPS

EPS = 1e-6
NEG = -1e30

SPLIT_SIZES = [D_A, D_KV, D_KV, D_A,
               D_B, D_B, D_B,
               2 * D_MODEL]
SPLIT_IDX = [int(s) for s in np.cumsum(SPLIT_SIZES)[:-1]]
D_IN = int(sum(SPLIT_SIZES))

kernel_name = "hybrid_swa_sink_gmlp_gated_block"


def _rms(x, g):
    xf = x.astype(jnp.float32)
    y = xf * lax.rsqrt(jnp.mean(xf * xf, axis=-1, keepdims=True) + EPS)
    return (y * g.astype(jnp.float32)).astype(x.dtype)


def _layernorm(x, g, b):
    xf = x.astype(jnp.float32)
    mu = jnp.mean(xf, axis=-1, keepdims=True)
    var = jnp.mean(jnp.square(xf - mu), axis=-1, keepdims=True)
    y = (xf - mu) * lax.rsqrt(var + EPS)
    return (y * g.astype(jnp.float32) + b.astype(jnp.float32)).astype(x.dtype)


def _rope(t, positions):
    half = HEAD_DIM // 2
    inv_freq = ROPE_THETA ** (-jnp.arange(half, dtype=jnp.float32) / half)
    ang = positions.astype(jnp.float32)[..., None] * inv_freq
    cos = jnp.cos(ang)[:, :, None, :]
    sin = jnp.sin(ang)[:, :, None, :]
    tf = t.astype(jnp.float32)
    t1, t2 = tf[..., :half], tf[..., half:]
    out = jnp.concatenate([t1 * cos - t2 * sin, t2 * cos + t1 * sin], axis=-1)
    return out.astype(t.dtype)


def _band(t):
    b, s, h, d = t.shape
    tb = t.reshape(b, s // BLOCK, BLOCK, h, d)
    prev = jnp.pad(tb[:, :-1], ((0, 0), (1, 0), (0, 0), (0, 0), (0, 0)))
    return jnp.concatenate([prev, tb], axis=2)


def _swa_with_sinks(q, k, v, sinks):
    b, s = q.shape[:2]
    nb = s // BLOCK
    qb = q.reshape(b, nb, BLOCK, N_KV_HEADS, GQ, HEAD_DIM)
    kb, vb = _band(k), _band(v)
    scores = jnp.einsum('bnqkgd,bnjkd->bnkgqj', qb, kb).astype(jnp.float32)
    scores = scores * (1.0 / math.sqrt(HEAD_DIM))
    qi = jnp.arange(BLOCK)[:, None]
    kj = jnp.arange(2 * BLOCK)[None, :]
    rel = qi + BLOCK - kj
    in_win = (rel >= 0) & (rel < WINDOW)
    blk = jnp.arange(nb)[:, None, None]
    valid = in_win[None] & ((blk > 0) | (kj[None] >= BLOCK))
    scores = jnp.where(valid[None, :, None, None], scores, NEG)
    sink = sinks.astype(jnp.float32).reshape(N_KV_HEADS, GQ)[None, None, :, :, None, None]
    sink = jnp.broadcast_to(sink, scores.shape[:-1] + (1,))
    probs = jax.nn.softmax(jnp.concatenate([scores, sink], axis=-1), axis=-1)[..., :-1]
    out = jnp.einsum('bnkgqj,bnjkd->bnqkgd', probs.astype(v.dtype), vb)
    return out.reshape(b, s, N_Q_HEADS * HEAD_DIM)


def _chunked_spatial_gating(u, v, ln_g, ln_b, w_s, b_s):
    b, s, _ = v.shape
    nc = s // CHUNK
    vn = _layernorm(v, ln_g, ln_b).reshape(b, nc, CHUNK, N_GROUPS, GROUP_W)
    causal = jnp.tril(jnp.ones((CHUNK, CHUNK), dtype=bool))
    w = jnp.where(causal[None], w_s, jnp.zeros((), w_s.dtype))
    sv = jnp.einsum('gts,bnsgc->bntgc', w, vn) + b_s.T[None, None, :, :, None]
    return u * sv.reshape(b, s, D_B)


def setup_inputs(seed: int = 0) -> dict:
    key = jax.random.key(seed)
    ks = jax.random.split(key, 18)
    f32 = jnp.float32
    nrm = lambda k, shape, s: jax.random.normal(k, shape, f32) * s
    x = jax.random.normal(ks[0], (BATCH, SEQ, D_MODEL), f32)
    c = jax.random.normal(ks[1], (BATCH, D_MODEL), f32)
    positions = jnp.broadcast_to(jnp.arange(SEQ, dtype=jnp.int32), (BATCH, SEQ))
    return {
        "x": x,
        "c": c,
        "positions": positions,
        "w_ada": nrm(ks[2], (DEPTH, D_MODEL, 3 * D_MODEL), 0.5 * D_MODEL ** -0.5),
        "b_ada": nrm(ks[3], (DEPTH, 3 * D_MODEL), 0.02),
        "g_pre": 1.0 + nrm(ks[4], (DEPTH, D_MODEL), 0.05),
        "g_post": 1.0 + nrm(ks[5], (DEPTH, D_MODEL), 0.05),
        "w_in": nrm(ks[6], (DEPTH, D_MODEL, D_IN), D_MODEL ** -0.5),
        "sinks": nrm(ks[7], (DEPTH, N_Q_HEADS), 1.0),
        "ln_v_g": 1.0 + nrm(ks[8], (DEPTH, D_B), 0.05),
        "ln_v_b": nrm(ks[9], (DEPTH, D_B), 0.02),
        "w_s": nrm(ks[10], (DEPTH, N_GROUPS, CHUNK, CHUNK), CHUNK ** -0.5),
        "b_s": 1.0 + nrm(ks[11], (DEPTH, N_GROUPS, CHUNK), 0.1),
        "w_proj_a": nrm(ks[12], (DEPTH, D_A, D_MODEL), D_A ** -0.5),
        "w_proj_b": nrm(ks[13], (DEPTH, D_B, D_MODEL), D_B ** -0.5),
        "w_out": nrm(ks[14], (DEPTH, D_MODEL, D_MODEL), D_MODEL ** -0.5),
    }


def reference(x, c, positions, w_ada, b_ada, g_pre, g_post, w_in, sinks,
              ln_v_g, ln_v_b, w_s, b_s, w_proj_a, w_proj_b, w_out):
    b, s, _ = x.shape
    c_act = jax.nn.silu(c)
    for l in range(DEPTH):
        ada = (c_act @ w_ada[l] + b_ada[l])[:, None, :]
        shift, scale, gate = jnp.split(ada, 3, axis=-1)
        h = _rms(x, g_pre[l]) * (1.0 + scale) + shift

        proj = h @ w_in[l]
        q, k, v, z_a, u_b, v_b, z_b, g_logits = jnp.split(proj, SPLIT_IDX, axis=-1)

        q = _rope(q.reshape(b, s, N_Q_HEADS, HEAD_DIM), positions)
        k = _rope(k.reshape(b, s, N_KV_HEADS, HEAD_DIM), positions)
        v = v.reshape(b, s, N_KV_HEADS, HEAD_DIM)
        y_a = _swa_with_sinks(q, k, v, sinks[l]) * jax.nn.silu(z_a)

        y_b = _chunked_spatial_gating(jax.nn.gelu(u_b, approximate=False),
                                      jax.nn.gelu(v_b, approximate=False),
                                      ln_v_g[l], ln_v_b[l], w_s[l], b_s[l])
        y_b = y_b * jax.nn.silu(z_b)

        gates = jax.nn.sigmoid(g_logits)
        gate_a, gate_b = jnp.split(gates, 2, axis=-1)
        merged = gate_a * (y_a @ w_proj_a[l]) + gate_b * (y_b @ w_proj_b[l])
        y = merged @ w_out[l]

        x = x + gate * _rms(y, g_post[l])
    return x
```

```python
import math
import sys
from contextlib import ExitStack

import numpy as np
import concourse.bass as bass
import concourse.mybir as mybir
from concourse.bass_utils import run_bass_kernel_spmd

F32 = mybir.dt.float32
BF16 = mybir.dt.bfloat16
I32 = mybir.dt.int32
AF = mybir.ActivationFunctionType
ALU = mybir.AluOpType

D = 1024
SEQ = 4096
NB = SEQ // 128
D_IN = 4864
EPS = 1e-6
TWO_PI = 2.0 * math.pi
CW1 = float(np.float32(6.28125))
CW2 = float(np.float32(TWO_PI - 6.28125))

SAME_ENGINE_STRICT = False

C_Q, C_K, C_V, C_ZA, C_ZB, C_VB, C_U, C_GA, C_GB = 0, 512, 640, 768, 1280, 1792, 2304, 2816, 3840


class Sem:
    def __init__(self, handle):
        self.handle = handle
        self.count = 0


class Buf:
    __slots__ = ("name", "w", "rs", "ro", "last_seq", "psum")

    def __init__(self, name, ro=False):
        self.name = name
        self.last_seq = -1
        self.psum = False
        self.w = None
        self.rs = {}
        self.ro = ro


class Op:
    __slots__ = ("eng", "idx", "fn", "deps", "dma", "sem", "val", "needed", "waits", "tag", "line")


class Prog:
    ENGS = ("pe", "act", "dve", "pool", "sp")

    def __init__(self):
        self.ops = {e: [] for e in self.ENGS}
        self.all_setup = []
        self.tag = "setup"
        self.seq = 0

    @staticmethod
    def _key(o):
        return ("d", id(o.sem)) if o.dma else ("e", o.eng)

    def op(self, eng, fn, reads=(), writes=(), dma_sem=None, after=()):
        o = Op()
        o.eng, o.fn, o.dma, o.needed, o.waits = eng, fn, dma_sem is not None, False, None
        o.idx = len(self.ops[eng])
        self.seq += 1
        for b in reads:
            b.last_seq = self.seq
        for b in writes:
            b.last_seq = self.seq
        o.tag = self.tag
        o.line = sys._getframe(1).f_lineno
        cand = []
        for b in reads:
            if b.w is not None:
                cand.append((b.w, True))
            if b.psum:
                for r in b.rs.values():
                    cand.append((r, False))
        for b in writes:
            for r in b.rs.values():
                cand.append((r, False))
            if b.w is not None:
                cand.append((b.w, False))
        for a in after:
            cand.append((a, True))
        best = {}
        for d, raw in cand:
            if d is o:
                continue
            if not d.dma and d.eng == eng and not o.dma:
                if eng == "pe" or (not raw and not SAME_ENGINE_STRICT):
                    continue
            k = self._key(d)
            cur = best.get(k)
            if cur is None or (d.dma and d.val > cur.val) or (not d.dma and d.idx > cur.idx):
                best[k] = d
        o.deps = list(best.values())
        for d in o.deps:
            d.needed = True
        if o.dma:
            dma_sem.count += 16
            o.sem, o.val = dma_sem, dma_sem.count
        else:
            o.sem, o.val = None, None
        for b in reads:
            if not b.ro:
                k = self._key(o)
                b.rs[k] = o
        for b in writes:
            b.w = o
            b.rs = {}
        self.ops[eng].append(o)
        return o

    def finalize(self, eng_sems):
        for e in self.ENGS:
            c = 0
            for o in self.ops[e]:
                if not o.dma:
                    o.sem = eng_sems[e]
                    if o.needed:
                        c += 1
                        o.val = c
            eng_sems[e].count = c
        for e in self.ENGS:
            waited = {}
            for o in self.ops[e]:
                req = {}
                for d in o.deps:
                    assert d.val is not None
                    k = id(d.sem)
                    if k not in req or req[k][1] < d.val:
                        req[k] = (d.sem, d.val)
                ws = []
                for k, (s, v) in req.items():
                    assert v <= s.count
                    if waited.get(k, 0) < v:
                        waited[k] = v
                        ws.append((s, v))
                o.waits = ws

    def emit(self, eng_name, e):
        for o in self.ops[eng_name]:
            for s, v in o.waits:
                e.wait_ge(s.handle, v)
            inst = o.fn(e)
            if o.dma:
                inst.then_inc(o.sem.handle, 16)
            elif o.needed:
                inst.then_inc(o.sem.handle, 1)


class T:
    def __init__(self, ap, name, ro=False, buf=None):
        self.ap = ap
        self.b = buf if buf is not None else Buf(name, ro)


def build_program(nblk=NB):
    nc = bass.Bass("TRN2", target_bir_lowering=False)
    P = Prog()

    def din(name, shape, dt=F32):
        return nc.dram_tensor(name, list(shape), dt, kind="ExternalInput").ap()

    x_d = din("x", [SEQ, D])
    y_d = nc.dram_tensor("y", [SEQ, D], F32, kind="ExternalOutput").ap()
    ccol_d = din("c_col", [128, 8])
    pos_d = din("pos_col", [128, NB], I32)
    wada_d = din("w_ada", [D, 3 * D])
    bada_d = din("b_ada", [3 * D])
    gpre_d = din("g_pre", [D])
    gpost_d = din("g_post", [D])
    win_d = din("w_in_p", [D, D_IN])
    sinks_d = din("sinks", [8])
    lng_d = din("ln_v_g", [512])
    lnb_d = din("ln_v_b", [512])
    wst_d = din("w_s_t", [128, 4, 128])
    bscol_d = din("b_s_col", [128, 4])
    wpa_d = din("w_pa_p", [512, D])
    wpb_d = din("w_proj_b", [512, D])
    wout_d = din("w_out", [D, D])
    ident_d = din("ident", [128, 128])
    masks_d = din("masks", [128, 2, 128])
    triu_d = din("triu", [128, 4, 128])
    invf_d = din("invf", [32])

    with ExitStack() as es:
        def sb(name, shape, dt):
            return es.enter_context(nc.sbuf_tensor(name, list(shape), dt))

        w_in_h = sb("w_in_sb", [128, 8, D_IN], BF16)
        w_pa_h = sb("w_pa_sb", [128, 4, D], BF16)
        w_pb_h = sb("w_pb_sb", [128, 4, D], BF16)
        w_out_h = sb("w_out_sb", [128, 8, D], BF16)
        gs_h = sb("gs_bc", [128, D], F32)
        sh_h = sb("shift_bc", [128, D], F32)
        gg_h = sb("gg_bc", [128, D], F32)
        ident_h = sb("ident_bf", [128, 128], BF16)
        masks_h = sb("masks_bf", [128, 2, 128], BF16)
        wT_h = sb("wT_bf", [128, 4, 128], BF16)
        bs_h = sb("bs_col", [128, 4], F32)
        esk_h = sb("esink_t", [128, 4, 128], F32)
        lg_h = sb("lg_bc", [128, 512], F32)
        lb_h = sb("lb_bc", [128, 512], F32)
        cos_h = sb("cosT", [128, NB, 32], F32)
        sin_h = sb("sinT", [128, NB, 32], F32)
        kT_h = sb("kT_ring", [128, 3, 128], BF16)
        va_h = sb("vaug_ring", [128, 3, 256], BF16)
        sm_h = sb("smalls", [128, 96], F32)
        nhalf_h = sb("nhalf", [128, 1], F32)
        posi_h = sb("posi", [128, NB], I32)

        SCR_WORDS = 17216
        scr_h = sb("scratch", [128, SCR_WORDS], F32)

        banks = []
        for i in range(8):
            h = es.enter_context(nc.psum_tensor("bank%d" % i, [128, 512], F32))
            banks.append(T(h[:, :], "bank%d" % i))
            banks[-1].b.psum = True
        bank_rr = [0]

        pinned = set()

        def next_bank():
            cands = [t for t in banks if id(t) not in pinned]
            assert cands, "all PSUM banks pinned"
            b = min(cands, key=lambda t: t.b.last_seq)
            P.seq += 1
            b.b.last_seq = P.seq
            return b

        def pin(*bs):
            for t in bs:
                pinned.add(id(t))

        def unpin(*bs):
            for t in bs:
                pinned.discard(id(t))

        eng_sems = {e: Sem(es.enter_context(nc.semaphore("s_" + e))) for e in ("pe", "act", "dve", "pool")}
        eng_sems["sp"] = Sem(es.enter_context(nc.semaphore("s_sp")))

        def dsem(name):
            return Sem(es.enter_context(nc.semaphore(name)))

        sem_small = dsem("d_small")
        sem_smB = dsem("d_smB")
        sem_smC = dsem("d_smC")
        sem_x = dsem("d_x")
        sem_o = dsem("d_o")
        sem_wada = [dsem("d_wada%d" % k) for k in range(8)]
        sem_win = [dsem("d_win%d" % k) for k in range(5)]
        sem_wp = [dsem("d_wp%d" % k) for k in range(3)]
        sem_bc = dsem("d_bc")
        sem_c = dsem("d_c")
        sem_xs = [dsem("d_xs0"), dsem("d_xs1")]
        sem_bada = [dsem("d_bada0"), dsem("d_bada1")]
        sem_bada2 = dsem("d_bada2")

        w_in = T(w_in_h[:, :, :], "w_in", ro=False)
        w_in_grp = [T(None, "w_in_g%d" % i, ro=True) for i in range(5)]
        w_pa = T(w_pa_h[:, :, :], "w_pa")
        w_pb = T(w_pb_h[:, :, :], "w_pb")
        w_out = T(w_out_h[:, :, :], "w_out")
        gs_bc = T(gs_h[:, :], "gs_bc")
        shift_bc = T(sh_h[:, :], "shift_bc")
        gg_bc = T(gg_h[:, :], "gg_bc")
        ident = T(ident_h[:, :], "ident", ro=True)
        masks = T(masks_h[:, :, :], "masks", ro=True)
        wT = T(wT_h[:, :, :], "wT")
        bs_col = T(bs_h[:, :], "bs_col", ro=True)
        esink_t = T(esk_h[:, :, :], "esink_t")
        lg_bc = T(lg_h[:, :], "lg_bc", ro=True)
        lb_bc = T(lb_h[:, :], "lb_bc", ro=True)
        cosT = T(cos_h[:, :, :], "cosT")
        sinT = T(sin_h[:, :, :], "sinT")
        kT_ring = [T(kT_h[:, s, :], "kT%d" % s) for s in range(3)]
        va_ring = [T(va_h[:, s, :], "va%d" % s) for s in range(3)]
        nhalf = T(nhalf_h[:, :], "nhalf")
        posi = T(posi_h[:, :], "posi", ro=True)

        sm_off = [0]

        def small(n, name):
            t = T(sm_h[:, sm_off[0]:sm_off[0] + n], name)
            sm_off[0] += n
            assert sm_off[0] <= 64
            return t

        c_col = small(8, "c_col")
        c_act = small(8, "c_act")
        sk_bc = small(8, "sk_bc")
        esk = small(8, "esk")
        ssq = small(1, "ssq")
        rs1 = small(1, "rs1")
        bnst = small(6, "bnst")
        bnmv = small(2, "bnmv")
        rs2 = small(1, "rs2")
        ssq2 = small(2, "ssq2")
        rs3 = small(1, "rs3")
        eps_col = small(1, "eps_col")
        posf = T(sm_h[:, 64:96], "posf")
        assert sm_off[0] <= 64

        def carve(off_words, nwords, dt, name, buf=None):
            ap = scr_h[:, off_words:off_words + nwords]
            if dt is BF16:
                ap = ap.bitcast(BF16)
            return T(ap, name, buf=buf)

        wada = [carve(k * 1536, 1536, BF16, "wada%d" % k) for k in range(8)]
        bada_bcs = [carve(12288, 1024, F32, "bada_bc0"), carve(15904, 1024, F32, "bada_bc1")]
        gpre_bc = carve(13312, 1024, F32, "gpre_bc")
        gpost_bc = carve(14336, 1024, F32, "gpost_bc")
        crep = carve(15360, 512, BF16, "crep")
        invf_bc = carve(15872, 32, F32, "invf_bc")
        wst_f = carve(15904, 0, F32, "dummy")
        setup_tiles = wada + bada_bcs + [gpre_bc, gpost_bc, crep, invf_bc]

        wo_f32 = w_out_h[:, :, :].rearrange("p a b -> p (a b)").bitcast(F32)

        def ropescr(i):
            return T(wo_f32[:, i * 1024:(i + 1) * 1024], "ropescr%d" % i, buf=w_out.b)

        def dma(eng, out_t, out_ap, in_ap, sem, reads=()):
            return P.op(eng, lambda e: e.dma_start(out=out_ap, in_=in_ap),
                        reads=[r.b for r in reads], writes=[out_t.b], dma_sem=sem)

        small_tokens = []

        small_groups = {"A": (sem_small, []), "B": (sem_smB, []), "C": (sem_smC, [])}

        def dma_small(out_t, in_ap, eng="sp", out_ap=None, grp="B"):
            sem_, lst = small_groups[grp]
            o = dma(eng, out_t, out_t.ap if out_ap is None else out_ap, in_ap, sem_)
            lst.append(o)
            return o

        x_stage = [T(w_pa_h[:, :, :].rearrange("p a b -> p (a b)").bitcast(F32)[:, 0:1024], "xst0", buf=w_pa.b),
                   T(w_pb_h[:, :, :].rearrange("p a b -> p (a b)").bitcast(F32)[:, 0:1024], "xst1", buf=w_pb.b)]
        dma("sp", c_col, c_col.ap, ccol_d[:, :], sem_c)
        for i_ in range(min(2, nblk)):
            dma("sp", x_stage[i_], x_stage[i_].ap, x_d[i_ * 128:(i_ + 1) * 128, :], sem_xs[i_])
        dma_small(posi, pos_d[:, :], grp="A")
        dma_small(invf_bc, invf_d.partition_broadcast(128), grp="A")
        dma_small(sk_bc, sinks_d.partition_broadcast(128), grp="A")
        dma_small(bs_col, bscol_d[:, :])
        dma_small(lg_bc, lng_d.partition_broadcast(128))
        dma_small(lb_bc, lnb_d.partition_broadcast(128))
        dma_small(gpre_bc, gpre_d.partition_broadcast(128))
        dma_small(gpost_bc, gpost_d.partition_broadcast(128))
        for t3_ in range(2):
            dma("sp", bada_bcs[t3_], bada_bcs[t3_].ap, bada_d[t3_ * 1024:(t3_ + 1) * 1024].partition_broadcast(128),
                sem_bada[t3_])
        bada2_pre = T(w_pb_h[:, :, :].rearrange("p a b -> p (a b)").bitcast(F32)[:, 1024:2048], "bada2", buf=w_pb.b)
        dma("sp", bada2_pre, bada2_pre.ap, bada_d[2048:3072].partition_broadcast(128), sem_bada2)
        dma_small(ident, ident_d[:, :], eng="pool", grp="C")
        dma_small(masks, masks_d[:, :, :], eng="pool", grp="C")
        for sem_, lst in small_groups.values():
            for o in lst:
                o.val = sem_.count

        for k in range(8):
            dma("pool", wada[k], wada[k].ap, wada_d[k * 128:(k + 1) * 128, :], sem_wada[k])

        grp_cols = [(0, 768), (768, 1792), (1792, 2816), (2816, 3840), (3840, 4864)]

        def load_win_group(i, after=None):
            c0, c1 = grp_cols[i]
            src = win_d[:, c0:c1].rearrange("(k p) n -> p k n", p=128)
            return P.op("pool", lambda e: e.dma_start(out=w_in_h[:, :, c0:c1], in_=src),
                        reads=[], writes=[w_in_grp[i].b], dma_sem=sem_win[i],
                        after=([] if after is None else [after.b.w]))

        def wgrp(col):
            for i, (c0, c1) in enumerate(grp_cols):
                if c0 <= col < c1:
                    return w_in_grp[i]
            raise AssertionError

        load_win_group(0, after=wada[5])
        load_win_group(1, after=wada[5])
        load_win_group(2, after=wada[7])

        P.op("act", lambda e: e.activation(out=c_act.ap, in_=c_col.ap, func=AF.Silu),
             reads=[c_col.b], writes=[c_act.b])
        crep3 = crep.ap.rearrange("p (k m) -> p k m", k=8)
        P.op("dve", lambda e: e.tensor_copy(out=crep3, in_=c_act.ap.unsqueeze(2).to_broadcast([128, 8, 128])),
             reads=[c_act.b], writes=[crep.b])
        P.op("act", lambda e: e.activation(out=esk.ap, in_=sk_bc.ap, func=AF.Exp),
             reads=[sk_bc.b], writes=[esk.b])
        P.op("dve", lambda e: e.tensor_copy(out=esk_h[64:128, :, :],
                                            in_=sm_h[64:128, 24:28].unsqueeze(2).to_broadcast([64, 4, 128])),
             reads=[esk.b], writes=[esink_t.b])
        P.op("dve", lambda e: e.tensor_copy(out=esk_h[0:64, :, :],
                                            in_=sm_h[0:64, 28:32].unsqueeze(2).to_broadcast([64, 4, 128])),
             reads=[esk.b, esink_t.b], writes=[esink_t.b])
        assert esk.ap.shape == (128, 8)
        P.op("pool", lambda e: e.memset(nhalf.ap, -0.5), writes=[nhalf.b])
        P.op("pool", lambda e: e.memset(eps_col.ap, EPS), writes=[eps_col.b])
        for s in range(3):
            P.op("pool", lambda e, s=s: e.memset(va_h[:, s, 64:192], 1.0), writes=[va_ring[s].b])

        ang, kf, rr, rc = ropescr(0), ropescr(1), ropescr(2), ropescr(3)
        ki_ap = kf.ap.bitcast(I32)
        wb = [w_out.b]
        P.op("dve", lambda e: e.tensor_copy(out=posf.ap, in_=posi.ap), reads=[posi.b], writes=[posf.b])
        P.op("dve", lambda e: e.tensor_tensor(
            out=ang.ap.rearrange("p (a b) -> p a b", a=NB),
            in0=posf.ap.unsqueeze(2).to_broadcast([128, NB, 32]),
            in1=invf_bc.ap.unsqueeze(1).to_broadcast([128, NB, 32]), op=ALU.mult),
            reads=[posf.b, invf_bc.b], writes=wb)
        P.op("dve", lambda e: e.tensor_scalar(out=rr.ap, in0=ang.ap, scalar1=1.0 / TWO_PI, scalar2=None, op0=ALU.mult),
             reads=wb, writes=wb)
        P.op("dve", lambda e: e.tensor_copy(out=ki_ap, in_=rr.ap), reads=wb, writes=wb)
        P.op("dve", lambda e: e.tensor_copy(out=rc.ap, in_=ki_ap), reads=wb, writes=wb)
        P.op("dve", lambda e: e.scalar_tensor_tensor(out=rr.ap, in0=rc.ap, scalar=-CW1, in1=ang.ap,
                                                     op0=ALU.mult, op1=ALU.add), reads=wb, writes=wb)
        P.op("dve", lambda e: e.scalar_tensor_tensor(out=rr.ap, in0=rc.ap, scalar=-CW2, in1=rr.ap,
                                                     op0=ALU.mult, op1=ALU.add), reads=wb, writes=wb)

        def wrap(t, tmp):
            P.op("dve", lambda e: e.tensor_scalar(out=tmp.ap, in0=t.ap, scalar1=math.pi, scalar2=-TWO_PI,
                                                  op0=ALU.is_gt, op1=ALU.mult), reads=wb, writes=wb)
            P.op("dve", lambda e: e.tensor_tensor(out=t.ap, in0=t.ap, in1=tmp.ap, op=ALU.add), reads=wb, writes=wb)
            P.op("dve", lambda e: e.tensor_scalar(out=tmp.ap, in0=t.ap, scalar1=-math.pi, scalar2=TWO_PI,
                                                  op0=ALU.is_lt, op1=ALU.mult), reads=wb, writes=wb)
            P.op("dve", lambda e: e.tensor_tensor(out=t.ap, in0=t.ap, in1=tmp.ap, op=ALU.add), reads=wb, writes=wb)
            P.op("dve", lambda e: e.tensor_scalar(out=t.ap, in0=t.ap, scalar1=-math.pi, scalar2=math.pi,
                                                  op0=ALU.max, op1=ALU.min), reads=wb, writes=wb)

        wrap(rr, rc)
        P.op("dve", lambda e: e.tensor_scalar(out=ang.ap, in0=rr.ap, scalar1=math.pi / 2, scalar2=None, op0=ALU.add),
             reads=wb, writes=wb)
        wrap(ang, rc)
        P.op("act", lambda e: e.activation(out=sin_h[:, :, :].rearrange("p a b -> p (a b)"), in_=rr.ap, func=AF.Sin),
             reads=wb, writes=[sinT.b])
        P.op("act", lambda e: e.activation(out=cos_h[:, :, :].rearrange("p a b -> p (a b)"), in_=ang.ap, func=AF.Sin),
             reads=wb, writes=[cosT.b])

        ada_banks = [next_bank() for _ in range(6)]
        for k in range(8):
            for nb in range(6):
                P.op("pe", lambda e, k=k, nb=nb: e.matmul(
                    ada_banks[nb].ap, crep3[:, k, :], wada[k].ap[:, nb * 512:(nb + 1) * 512],
                    start=(k == 0), stop=(k == 7)),
                    reads=[crep.b, wada[k].b], writes=[ada_banks[nb].b])
        bada_loads = {}
        bada2 = T(w_pb_h[:, :, :].rearrange("p a b -> p (a b)").bitcast(F32)[:, 1024:2048], "bada2", buf=w_pb.b)

        def load_bada(t3):
            bb = bada_bcs[t3 % 2]
            bada_loads[t3] = dma("sp", bb, bb.ap, bada_d[t3 * 1024:(t3 + 1) * 1024].partition_broadcast(128), sem_bada[t3 % 2])

        for t3 in (1, 0, 2):
            bada_bc = bada_bcs[t3 % 2] if t3 < 2 else bada2
            for h in range(2):
                bk = ada_banks[t3 * 2 + h]
                sl = slice(h * 512, (h + 1) * 512)
                if t3 == 0:
                    P.op("dve", lambda e, bk=bk, sl=sl, bb=bada_bc: e.tensor_tensor(out=sh_h[:, sl], in0=bk.ap, in1=bb.ap[:, sl],
                                                                          op=ALU.add),
                         reads=[bk.b, bada_bc.b], writes=[shift_bc.b])
                elif t3 == 1:
                    P.op("dve", lambda e, bk=bk, sl=sl, bb=bada_bc: e.scalar_tensor_tensor(
                        out=gs_h[:, sl], in0=bk.ap, scalar=1.0, in1=bb.ap[:, sl], op0=ALU.add, op1=ALU.add),
                        reads=[bk.b, bada_bc.b], writes=[gs_bc.b])
                    P.op("dve", lambda e, sl=sl: e.tensor_tensor(out=gs_h[:, sl], in0=gs_h[:, sl], in1=gpre_bc.ap[:, sl],
                                                                  op=ALU.mult),
                         reads=[gs_bc.b, gpre_bc.b], writes=[gs_bc.b])
                else:
                    P.op("dve", lambda e, bk=bk, sl=sl, bb=bada_bc: e.tensor_tensor(out=gg_h[:, sl], in0=bk.ap, in1=bb.ap[:, sl],
                                                                          op=ALU.add),
                         reads=[bk.b, bada_bc.b], writes=[gg_bc.b])
                    P.op("dve", lambda e, sl=sl: e.tensor_tensor(out=gg_h[:, sl], in0=gg_h[:, sl], in1=gpost_bc.ap[:, sl],
                                                                  op=ALU.mult),
                         reads=[gg_bc.b, gpost_bc.b], writes=[gg_bc.b])

        wsf, trf = ropescr(2), ropescr(3)
        o1_ = dma("sp", wsf, wsf.ap[:, 0:512], wst_d.rearrange("p a b -> p (a b)"), sem_bc)
        o2_ = dma("sp", trf, trf.ap[:, 0:512], triu_d.rearrange("p a b -> p (a b)"), sem_bc)
        P.op("dve", lambda e: e.tensor_tensor(out=wT_h[:, :, :].rearrange("p a b -> p (a b)"),
                                              in0=wsf.ap[:, 0:512], in1=trf.ap[:, 0:512], op=ALU.mult),
             reads=wb, writes=[wT.b])

        def load_remaining_weights():
            load_win_group(3)
            load_win_group(4)
            P.op("pool", lambda e: e.dma_start(out=w_pa_h[:, :, :], in_=wpa_d.rearrange("(k p) n -> p k n", p=128)),
                 writes=[w_pa.b], dma_sem=sem_wp[0])
            P.op("pool", lambda e: e.dma_start(out=w_pb_h[:, :, :], in_=wpb_d.rearrange("(k p) n -> p k n", p=128)),
                 writes=[w_pb.b], dma_sem=sem_wp[1])
            P.op("pool", lambda e: e.dma_start(out=w_out_h[:, :, :], in_=wout_d.rearrange("(k p) n -> p k n", p=128)),
                 writes=[w_out.b], dma_sem=sem_wp[2])
            w_out.b.ro = True
            w_pa.b.ro = True
            w_pb.b.ro = True


        alias_rs = {}
        alias_ws = []
        for t in setup_tiles:
            for k_, r in t.b.rs.items():
                alias_rs[(k_, id(t))] = r
            if t.b.w is not None:
                alias_ws.append(t.b.w)

        moff = [0]

        def mt(nwords, dt, name):
            nwords = (nwords + 7) // 8 * 8
            ap = scr_h[:, moff[0]:moff[0] + nwords]
            moff[0] += nwords
            assert moff[0] <= SCR_WORDS, (name, moff[0])
            if dt is BF16:
                ap = ap.bitcast(BF16)
            t = T(ap, name)
            i = 0
            for r in list(alias_rs.values()) + alias_ws:
                t.b.rs[("alias", i)] = r
                i += 1
            return t

        x_in = mt(1024, F32, "x_in")
        x_in_main = x_in
        hp = mt(1024, F32, "hp")
        h_bf = [mt(512, BF16, "h_bf%d" % i) for i in range(2)]
        hT = [mt(512, BF16, "hT%d" % i) for i in range(2)]
        tA = mt(512, F32, "tA")
        tB = mt(512, F32, "tB")
        tAk = mt(128, F32, "tAk")
        tBk = mt(128, F32, "tBk")
        qk_r = mt(320, BF16, "qk_r")
        qT = [mt(256, BF16, "qT%d" % i) for i in range(2)]
        sza = mt(256, BF16, "sza")
        szaT = [mt(256, BF16, "szaT%d" % i) for i in range(2)]
        Et = [mt(256, BF16, "E%d" % i) for i in range(4)]
        Pm = [mt(256, BF16, "Pm%d" % i) for i in range(4)]
        den = mt(512, F32, "den")
        ytmp = mt(512, F32, "ytmp")
        yaT = mt(256, BF16, "yaT")
        gv = mt(512, F32, "gv")
        vn_bf = [mt(256, BF16, "vn_bf%d" % i) for i in range(2)]
        su = [mt(256, BF16, "su%d" % i) for i in range(2)]
        szb = mt(256, BF16, "szb")
        sgz = mt(256, BF16, "sgz")
        sgz2 = mt(256, BF16, "sgz2")
        yb = mt(256, BF16, "yb")
        ybT = mt(256, BF16, "ybT")
        sig = mt(1024, BF16, "sig")
        m1 = mt(512, F32, "m1")
        m2 = mt(512, F32, "m2")
        merged = mt(512, BF16, "merged")
        mTt = mt(512, BF16, "mT")
        outt = mt(1024, F32, "out")

        w_in_ap = w_in_h

        def mm_group(outs, lhs_fn, K, reads_fn, mid=None):
            for k in range(K):
                if mid is not None and k == K // 2:
                    mid()
                for (bk, width, rhs_fn, extra) in outs:
                    P.op("pe", lambda e, k=k, bk=bk, width=width, rhs_fn=rhs_fn: e.matmul(
                        bk.ap[:, 0:width], lhs_fn(k), rhs_fn(k), start=(k == 0), stop=(k == K - 1)),
                        reads=reads_fn(k) + extra, writes=[bk.b])

        def XLOAD(n):
            P.tag = "XL(%d)" % n
            if n < 2:
                return
            dma("act", x_in, x_in.ap, x_d[n * 128:(n + 1) * 128, :], sem_x)

        def S0a(n):
            P.tag = "S0a(%d)" % n
            x_in = x_in_main if n >= 2 else x_stage[n]
            P.op("act", lambda e: e.activation(out=hp.ap, in_=x_in.ap, func=AF.Square, accum_out=ssq.ap),
                 reads=[x_in.b], writes=[hp.b, ssq.b])
            P.op("dve", lambda e: e.tensor_scalar(out=rs1.ap, in0=ssq.ap, scalar1=1.0 / D, scalar2=EPS,
                                                  op0=ALU.mult, op1=ALU.add), reads=[ssq.b], writes=[rs1.b])
            P.op("pool", lambda e: e.tensor_tensor(out=rs1.ap, in0=rs1.ap, in1=nhalf.ap, op=ALU.pow),
                 reads=[rs1.b, nhalf.b], writes=[rs1.b])

        def S0b(n):
            P.tag = "S0b(%d)" % n
            hb = h_bf[n % 2]
            x_in = x_in_main if n >= 2 else x_stage[n]
            P.op("dve", lambda e: e.scalar_tensor_tensor(out=hp.ap, in0=x_in.ap, scalar=rs1.ap, in1=gs_bc.ap,
                                                         op0=ALU.mult, op1=ALU.mult),
                 reads=[x_in.b, rs1.b, gs_bc.b], writes=[hp.b])
            P.op("dve", lambda e: e.tensor_tensor(out=hb.ap, in0=hp.ap, in1=shift_bc.ap, op=ALU.add),
                 reads=[hp.b, shift_bc.b], writes=[hb.b])

        def S0(n):
            S0a(n)
            S0b(n)

        def F1(n):
            P.tag = "F1(%d)" % n
            hb, ht = h_bf[n % 2], hT[n % 2]
            bk = next_bank()
            pb = bk.ap.bitcast(BF16)
            for k in range(8):
                P.op("pe", lambda e, k=k: e.transpose(pb[:, k * 128:(k + 1) * 128], hb.ap[:, k * 128:(k + 1) * 128], ident.ap),
                     reads=[hb.b, ident.b], writes=[bk.b])
            P.op("act", lambda e: e.activation(out=ht.ap, in_=pb, func=AF.Copy), reads=[bk.b], writes=[ht.b])

        def proj_banks(n, specs, mid=None):
            ht = hT[n % 2]
            ht3 = ht.ap.rearrange("p (k m) -> p k m", k=8)
            outs = []
            for (c0, width) in specs:
                bk = next_bank()
                outs.append((bk, width, (lambda k, c0=c0, width=width: w_in_ap[:, k, c0:c0 + width]), [wgrp(c0).b]))
            mm_group(outs, lambda k: ht3[:, k, :], 8, lambda k: [ht.b], mid=mid)
            return [o[0] for o in outs]

        def rope(src_bank, width, nh, dst_off, n, tA, tB):
            src = src_bank.ap[:, 0:width]
            s4 = src.rearrange("p (h t f) -> p h t f", h=nh, t=2, f=32)
            s3 = src.rearrange("p (h f) -> p h f", h=2 * nh, f=32)
            a3 = tA.ap[:, 0:width].rearrange("p (h f) -> p h f", h=2 * nh, f=32)
            b4 = tB.ap[:, 0:width].rearrange("p (h t f) -> p h t f", h=nh, t=2, f=32)
            cs = cos_h[:, n, :].unsqueeze(1).to_broadcast([128, 2 * nh, 32])
            sn = sin_h[:, n, :].unsqueeze(1).to_broadcast([128, nh, 32])
            P.op("dve", lambda e: e.tensor_tensor(out=a3, in0=s3, in1=cs, op=ALU.mult),
                 reads=[src_bank.b, cosT.b], writes=[tA.b])
            P.op("dve", lambda e: e.scalar_tensor_tensor(out=b4[:, :, 0, :], in0=s4[:, :, 1, :], scalar=-1.0, in1=sn,
                                                         op0=ALU.mult, op1=ALU.mult),
                 reads=[src_bank.b, sinT.b], writes=[tB.b])
            P.op("dve", lambda e: e.tensor_tensor(out=b4[:, :, 1, :], in0=s4[:, :, 0, :], in1=sn, op=ALU.mult),
                 reads=[src_bank.b, sinT.b], writes=[tB.b])
            P.op("pool", lambda e: e.tensor_tensor(out=qk_r.ap[:, dst_off:dst_off + width],
                                                   in0=tA.ap[:, 0:width],
                                                   in1=tB.ap[:, 0:width], op=ALU.add),
                 reads=[tA.b, tB.b], writes=[qk_r.b])

        za_bank = {}
        qk_bank = {}

        def FR1(n, mid=None):
            P.tag = "FR1(%d)" % n
            tag_ = P.tag

            def mid_():
                if mid is not None:
                    mid()
                    P.tag = tag_

            bq, bkv, bza = proj_banks(n, [(C_Q, 512), (C_K, 256), (C_ZA, 512)], mid=mid_)
            va = va_ring[n % 3]
            va4 = va.ap.rearrange("p (a b) -> p a b", a=4)
            P.op("act", lambda e: e.activation(out=va4[:, 0:4:3, :], in_=bkv.ap[:, 128:256].rearrange("p (a b) -> p a b", a=2),
                                               func=AF.Copy), reads=[bkv.b], writes=[va.b])
            za_bank[n] = bza
            qk_bank[n] = (bq, bkv)
            pin(bza, bq, bkv)

        def FR1rope(n):
            P.tag = "FR1rope(%d)" % n
            bq, bkv = qk_bank.pop(n)
            a3 = tA.ap[:, 0:512].rearrange("p (h f) -> p h f", h=16, f=32)
            a4 = tA.ap[:, 0:512].rearrange("p (h t f) -> p h t f", h=8, t=2, f=32)
            b4 = tB.ap[:, 0:512].rearrange("p (h t f) -> p h t f", h=8, t=2, f=32)
            o4 = qk_r.ap[:, 0:512].rearrange("p (h t f) -> p h t f", h=8, t=2, f=32)
            cs = cos_h[:, n, :].unsqueeze(1).to_broadcast([128, 16, 32])
            sn = sin_h[:, n, :].unsqueeze(1).to_broadcast([128, 8, 32])
            P.op("dve", lambda e: e.tensor_copy(out=tA.ap[:, 0:512], in_=bq.ap[:, 0:512]), reads=[bq.b], writes=[tA.b])
            rope(bkv, 128, 2, 512, n, tAk, tBk)
            P.op("pool", lambda e: e.tensor_tensor(out=b4[:, :, 0, :], in0=a4[:, :, 1, :], in1=sn, op=ALU.mult),
                 reads=[tA.b, sinT.b], writes=[tB.b])
            P.op("pool", lambda e: e.tensor_tensor(out=b4[:, :, 1, :], in0=a4[:, :, 0, :], in1=sn, op=ALU.mult),
                 reads=[tA.b, sinT.b], writes=[tB.b])
            P.op("pool", lambda e: e.tensor_tensor(out=a3, in0=a3, in1=cs, op=ALU.mult),
                 reads=[tA.b, cosT.b], writes=[tA.b])
            P.op("pool", lambda e: e.tensor_tensor(out=o4[:, :, 0, :], in0=a4[:, :, 0, :], in1=b4[:, :, 0, :], op=ALU.subtract),
                 reads=[tA.b, tB.b], writes=[qk_r.b])
            P.op("pool", lambda e: e.tensor_tensor(out=o4[:, :, 1, :], in0=a4[:, :, 1, :], in1=b4[:, :, 1, :], op=ALU.add),
                 reads=[tA.b, tB.b], writes=[qk_r.b])
            unpin(bq, bkv)

        def FR1b(n):
            P.tag = "FR1b(%d)" % n
            bza = za_bank.pop(n)
            P.op("act", lambda e: e.activation(out=sgz.ap, in_=bza.ap, func=AF.Sigmoid), reads=[bza.b], writes=[sgz.b])
            P.op("dve", lambda e: e.tensor_tensor(out=sza.ap, in0=bza.ap, in1=sgz.ap, op=ALU.mult),
                 reads=[bza.b, sgz.b], writes=[sza.b])
            unpin(bza)

        fr2_banks = {}

        def FR2(n):
            P.tag = "FR2(%d)" % n
            sl = n % 2
            fr2_banks[n] = proj_banks(n, [(C_ZB, 512), (C_VB, 512), (C_U, 512)])
            pin(*fr2_banks[n])

        def FR2act(n):
            P.tag = "FR2act(%d)" % n
            sl = n % 2
            bzb, bvb, bu = fr2_banks[n]
            P.op("act", lambda e: e.activation(out=sgz2.ap, in_=bzb.ap, func=AF.Sigmoid), reads=[bzb.b], writes=[sgz2.b])
            P.op("act", lambda e: e.activation(out=gv.ap, in_=bvb.ap, func=AF.Gelu), reads=[bvb.b], writes=[gv.b])
            P.op("act", lambda e: e.activation(out=su[sl].ap, in_=bu.ap, func=AF.Gelu), reads=[bu.b], writes=[su[sl].b])

        def FR2ev(n):
            P.tag = "FR2ev(%d)" % n
            sl = n % 2
            bzb, bvb, bu = fr2_banks.pop(n)
            unpin(bzb, bvb, bu)
            P.op("dve", lambda e: e.tensor_tensor(out=szb.ap, in0=bzb.ap, in1=sgz2.ap, op=ALU.mult),
                 reads=[bzb.b, sgz2.b], writes=[szb.b])
            P.op("dve", lambda e: e.bn_stats(out=bnst.ap, in_=gv.ap), reads=[gv.b], writes=[bnst.b])
            P.op("dve", lambda e: e.bn_aggr(out=bnmv.ap, in_=bnst.ap), reads=[bnst.b], writes=[bnmv.b])
            P.op("dve", lambda e: e.tensor_scalar(out=rs2.ap, in0=bnmv.ap[:, 1:2], scalar1=EPS, scalar2=None, op0=ALU.add),
                 reads=[bnmv.b], writes=[rs2.b])
            P.op("pool", lambda e: e.tensor_tensor(out=rs2.ap, in0=rs2.ap, in1=nhalf.ap, op=ALU.pow),
                 reads=[rs2.b, nhalf.b], writes=[rs2.b])
            P.op("dve", lambda e: e.tensor_scalar(out=gv.ap, in0=gv.ap, scalar1=bnmv.ap[:, 0:1], scalar2=rs2.ap,
                                                  op0=ALU.subtract, op1=ALU.mult),
                 reads=[gv.b, bnmv.b, rs2.b], writes=[gv.b])
            P.op("pool", lambda e: e.tensor_tensor(out=gv.ap, in0=gv.ap, in1=lg_bc.ap, op=ALU.mult),
                 reads=[gv.b, lg_bc.b], writes=[gv.b])
            P.op("pool", lambda e: e.tensor_tensor(out=vn_bf[sl].ap, in0=gv.ap, in1=lb_bc.ap, op=ALU.add),
                 reads=[gv.b, lb_bc.b], writes=[vn_bf[sl].b])
            P.op("pool", lambda e: e.tensor_tensor(out=su[sl].ap, in0=su[sl].ap, in1=szb.ap, op=ALU.mult),
                 reads=[su[sl].b, szb.b], writes=[su[sl].b])

        def FR3(n):
            P.tag = "FR3(%d)" % n
            sl = n % 2
            bt = next_bank()
            pbt = bt.ap.bitcast(BF16)
            for j in range(4):
                P.op("pe", lambda e, j=j: e.transpose(pbt[:, j * 128:(j + 1) * 128], qk_r.ap[:, j * 128:(j + 1) * 128], ident.ap),
                     reads=[qk_r.b, ident.b], writes=[bt.b])
            for j in range(4):
                P.op("pe", lambda e, j=j: e.transpose(pbt[:, 512 + j * 128:512 + (j + 1) * 128],
                                                      sza.ap[:, j * 128:(j + 1) * 128], ident.ap),
                     reads=[sza.b, ident.b], writes=[bt.b])
            P.op("dve", lambda e: e.tensor_copy(out=qT[sl].ap, in_=pbt[:, 0:512]), reads=[bt.b], writes=[qT[sl].b])
            P.op("act", lambda e: e.activation(out=szaT[sl].ap, in_=pbt[:, 512:1024], func=AF.Copy),
                 reads=[bt.b], writes=[szaT[sl].b])

        def KT(n, bt=None, col0=0):
            if bt is None:
                bt = next_bank()
            pbt = bt.ap.bitcast(BF16)
            P.op("pe", lambda e: e.transpose(pbt[:, col0:col0 + 128], qk_r.ap[:, 512:640], ident.ap),
                 reads=[qk_r.b, ident.b], writes=[bt.b])
            P.op("dve", lambda e: e.tensor_copy(out=kT_ring[n % 3].ap, in_=pbt[:, col0:col0 + 128]), reads=[bt.b],
                 writes=[kT_ring[n % 3].b])

        def SP(n):
            P.tag = "SP(%d)" % n
            sl = n % 2
            bsv = next_bank()
            for g in range(4):
                P.op("pe", lambda e, g=g: e.matmul(bsv.ap[:, g * 128:(g + 1) * 128], wT_h[:, g, :],
                                                   vn_bf[sl].ap[:, g * 128:(g + 1) * 128], start=True, stop=True),
                     reads=[wT.b, vn_bf[sl].b], writes=[bsv.b])
            for g in range(4):
                P.op("dve", lambda e, g=g: e.scalar_tensor_tensor(
                    out=yb.ap[:, g * 128:(g + 1) * 128], in0=bsv.ap[:, g * 128:(g + 1) * 128], scalar=bs_h[:, g:g + 1],
                    in1=su[sl].ap[:, g * 128:(g + 1) * 128], op0=ALU.add, op1=ALU.mult),
                    reads=[bsv.b, bs_col.b, su[sl].b], writes=[yb.b])

        sc_state = {}

        def SC(n, kvhs=(0, 1)):
            P.tag = "SC(%d)" % n
            sl = n % 2
            kbs = ([] if n == 0 else [(n - 1, 0)]) + [(n, 1)]
            q3 = qT[sl].ap
            st = sc_state.setdefault(n, {"pm": {}, "ei": 0})
            for kvh in kvhs:
                ps_ = slice(kvh * 64, (kvh + 1) * 64)
                for (kb, mi) in kbs:
                    bs_ = next_bank()
                    kt = kT_ring[kb % 3]
                    P.op("pe", lambda e, bs_=bs_, kt=kt, ps_=ps_: e.matmul(bs_.ap, kt.ap[ps_, :], q3[ps_, :], start=True, stop=True),
                         reads=[kt.b, qT[sl].b], writes=[bs_.b])
                    et = Et[st["ei"] % 4]
                    pm = Pm[kvh * 2 + mi]
                    st["ei"] += 1
                    P.op("act", lambda e, bs_=bs_, et=et: e.activation(out=et.ap, in_=bs_.ap, func=AF.Exp, scale=0.125),
                         reads=[bs_.b], writes=[et.b])
                    P.op("dve", lambda e, et=et, pm=pm, mi=mi: e.tensor_tensor(
                        out=pm.ap.rearrange("p (g q) -> p g q", g=4), in0=et.ap.rearrange("p (g q) -> p g q", g=4),
                        in1=masks_h[:, mi, :].unsqueeze(1).to_broadcast([128, 4, 128]), op=ALU.mult),
                         reads=[et.b, masks.b], writes=[pm.b])
                    st["pm"][(kvh, mi)] = (pm, kb)
            return kbs, st["pm"]

        gt_banks = {}

        def GT(n):
            P.tag = "GT(%d)" % n
            gt_banks[n] = proj_banks(n, [(C_GA, 512), (C_GA + 512, 512), (C_GB, 512), (C_GB + 512, 512)])
            pin(*gt_banks[n])

        def GTact(n):
            P.tag = "GTact(%d)" % n
            gbs_ = gt_banks.pop(n)
            unpin(*gbs_)
            for i, bk in enumerate(gbs_):
                off = i * 512
                P.op("act", lambda e, bk=bk, off=off: e.activation(out=sig.ap[:, off:off + 512], in_=bk.ap, func=AF.Sigmoid),
                     reads=[bk.b], writes=[sig.b])

        pv_banks = {}

        def PV(n, kbs, pm_list):
            P.tag = "PV(%d)" % n
            sl = n % 2
            po = []
            for kvh in range(2):
                bo = next_bank()
                for i, (kb, mi) in enumerate(kbs):
                    pm, _ = pm_list[(kvh, mi)]
                    va = va_ring[kb % 3]
                    P.op("pe", lambda e, bo=bo, va=va, pm=pm, i=i, kvh=kvh: e.matmul(
                        bo.ap, va.ap[:, kvh * 128:(kvh + 1) * 128], pm.ap, start=(i == 0), stop=(i == len(kbs) - 1)),
                        reads=[va.b, pm.b], writes=[bo.b])
                po.append(bo)
            esk2 = esk_h[:, :, :].rearrange("p a b -> p (a b)")
            for kvh in range(2):
                s_rows = slice(64, 128) if kvh == 0 else slice(0, 64)
                bo = po[kvh]
                P.op("dve", lambda e, bo=bo, s_rows=s_rows: e.tensor_tensor(out=den.ap[s_rows, :], in0=bo.ap[s_rows, :],
                                                                              in1=esk2[s_rows, :], op=ALU.add),
                     reads=[bo.b, esink_t.b], writes=[den.b])
            P.op("act", lambda e: e.activation(out=den.ap, in_=den.ap, func=AF.Ln), reads=[den.b], writes=[den.b])
            P.op("act", lambda e: e.activation(out=den.ap, in_=den.ap, func=AF.Exp, scale=-1.0), reads=[den.b], writes=[den.b])
            pv_banks[n] = po
            pin(*po)

        def PVb(n):
            P.tag = "PVb(%d)" % n
            sl = n % 2
            po = pv_banks.pop(n)
            unpin(*po)
            for kvh in range(2):
                o_rows = slice(0, 64) if kvh == 0 else slice(64, 128)
                s_rows = slice(64, 128) if kvh == 0 else slice(0, 64)
                bo = po[kvh]
                P.op("dve", lambda e, bo=bo, s_rows=s_rows, o_rows=o_rows: e.tensor_tensor(
                    out=ytmp.ap[o_rows, :], in0=bo.ap[o_rows, :], in1=den.ap[s_rows, :], op=ALU.mult),
                    reads=[bo.b, den.b], writes=[ytmp.b])
            P.op("pool", lambda e: e.tensor_tensor(out=yaT.ap, in0=ytmp.ap, in1=szaT[sl].ap, op=ALU.mult),
                 reads=[ytmp.b, szaT[sl].b], writes=[yaT.b])

        def YBT(n, with_k_of=None):
            P.tag = "YBT(%d)" % n
            bt = next_bank()
            pbt = bt.ap.bitcast(BF16)
            for j in range(4):
                P.op("pe", lambda e, j=j: e.transpose(pbt[:, j * 128:(j + 1) * 128], yb.ap[:, j * 128:(j + 1) * 128], ident.ap),
                     reads=[yb.b, ident.b], writes=[bt.b])
            if with_k_of is not None:
                P.op("pe", lambda e: e.transpose(pbt[:, 512:640], qk_r.ap[:, 512:640], ident.ap),
                     reads=[qk_r.b, ident.b], writes=[bt.b])
            P.op("dve", lambda e: e.tensor_copy(out=ybT.ap, in_=pbt[:, 0:512]), reads=[bt.b], writes=[ybT.b])
            if with_k_of is not None:
                kt = kT_ring[with_k_of % 3]
                P.op("dve", lambda e: e.tensor_copy(out=kt.ap, in_=pbt[:, 512:640]), reads=[bt.b], writes=[kt.b])

        def PAB(n):
            P.tag = "PAB(%d)" % n
            ya3 = yaT.ap.rearrange("p (g m) -> p g m", g=4)
            yb3 = ybT.ap.rearrange("p (g m) -> p g m", g=4)
            pab = [next_bank(), next_bank()]
            mm_group([(pab[h], 512, (lambda k, h=h: w_pa_h[:, k, h * 512:(h + 1) * 512]), [w_pa.b]) for h in range(2)],
                     lambda k: ya3[:, k, :], 4, lambda k: [yaT.b])
            pbb = [next_bank(), next_bank()]
            mm_group([(pbb[h], 512, (lambda k, h=h: w_pb_h[:, k, h * 512:(h + 1) * 512]), [w_pb.b]) for h in range(2)],
                     lambda k: yb3[:, k, :], 4, lambda k: [ybT.b])
            for h in range(2):
                sl_ = slice(h * 512, (h + 1) * 512)
                mm = m1 if h == 0 else m2
                P.op("dve", lambda e, h=h, mm=mm, sl_=sl_: e.tensor_tensor(out=mm.ap, in0=pab[h].ap,
                                                                  in1=sig.ap[:, sl_], op=ALU.mult),
                     reads=[pab[h].b, sig.b], writes=[mm.b])
            for h in range(2):
                sl_ = slice(h * 512, (h + 1) * 512)
                mm = m1 if h == 0 else m2
                tmp_ = ytmp if h == 0 else den
                P.op("dve", lambda e, h=h, sl_=sl_, tmp_=tmp_: e.tensor_tensor(out=tmp_.ap, in0=pbb[h].ap,
                                                                    in1=sig.ap[:, 1024 + h * 512:1536 + h * 512], op=ALU.mult),
                     reads=[pbb[h].b, sig.b], writes=[tmp_.b])
                P.op("pool", lambda e, sl_=sl_, mm=mm, tmp_=tmp_: e.tensor_tensor(out=merged.ap[:, sl_], in0=tmp_.ap, in1=mm.ap,
                                                                               op=ALU.add),
                     reads=[tmp_.b, mm.b], writes=[merged.b])

        def MT(n):
            P.tag = "MT(%d)" % n
            dma("act", outt, outt.ap, x_d[n * 128:(n + 1) * 128, :], sem_o)
            bt = next_bank()
            pbt = bt.ap.bitcast(BF16)
            for j in range(8):
                P.op("pe", lambda e, j=j: e.transpose(pbt[:, j * 128:(j + 1) * 128], merged.ap[:, j * 128:(j + 1) * 128], ident.ap),
                     reads=[merged.b, ident.b], writes=[bt.b])
            P.op("dve", lambda e: e.tensor_copy(out=mTt.ap, in_=pbt), reads=[bt.b], writes=[mTt.b])

        def WO(n):
            P.tag = "WO(%d)" % n
            m3 = mTt.ap.rearrange("p (k m) -> p k m", k=8)
            pyb = [next_bank(), next_bank()]
            mm_group([(pyb[h], 512, (lambda k, h=h: w_out_h[:, k, h * 512:(h + 1) * 512]), [w_out.b]) for h in range(2)],
                     lambda k: m3[:, k, :], 8, lambda k: [mTt.b])
            for h in range(2):
                jk = sgz if h == 0 else sgz2
                P.op("act", lambda e, h=h, jk=jk: e.activation(out=jk.ap, in_=pyb[h].ap, func=AF.Square,
                                                               accum_out=ssq2.ap[:, h:h + 1]),
                     reads=[pyb[h].b], writes=[jk.b, ssq2.b])
            for h in range(2):
                sl_ = slice(h * 512, (h + 1) * 512)
                mm = m1 if h == 0 else m2
                P.op("dve", lambda e, h=h, sl_=sl_, mm=mm: e.tensor_tensor(out=mm.ap, in0=pyb[h].ap, in1=gg_h[:, sl_], op=ALU.mult),
                     reads=[pyb[h].b, gg_bc.b], writes=[mm.b])
            P.op("dve", lambda e: e.tensor_tensor(out=rs3.ap, in0=ssq2.ap[:, 0:1], in1=ssq2.ap[:, 1:2], op=ALU.add),
                 reads=[ssq2.b], writes=[rs3.b])

        def WOfin(n):
            P.tag = "WOfin(%d)" % n
            P.op("act", lambda e: e.activation(out=rs3.ap, in_=rs3.ap, func=AF.Ln, scale=1.0 / D, bias=eps_col.ap),
                 reads=[rs3.b, eps_col.b], writes=[rs3.b])
            P.op("act", lambda e: e.activation(out=rs3.ap, in_=rs3.ap, func=AF.Exp, scale=-0.5),
                 reads=[rs3.b], writes=[rs3.b])
            for h in range(2):
                sl_ = slice(h * 512, (h + 1) * 512)
                mm = m1 if h == 0 else m2
                P.op("dve", lambda e, sl_=sl_, mm=mm: e.scalar_tensor_tensor(
                    out=outt.ap[:, sl_], in0=mm.ap, scalar=rs3.ap, in1=outt.ap[:, sl_], op0=ALU.mult, op1=ALU.add),
                    reads=[mm.b, rs3.b, outt.b], writes=[outt.b])
            return P.op("sp", lambda e: e.dma_start(out=y_d[n * 128:(n + 1) * 128, :], in_=outt.ap),
                        reads=[outt.b], writes=[], dma_sem=sem_o)

        def WARM(count):
            P.tag = "WARM"
            bk = next_bank()
            for _ in range(count):
                P.op("pe", lambda e: e.matmul(bk.ap, ident.ap, w_in_h[:, 0, 0:512], start=True, stop=True),
                     reads=[ident.b, w_in_grp[0].b], writes=[bk.b])

        XLOAD(0)
        S0(0)
        if nblk > 1:
            XLOAD(1)
            S0(1)
        load_remaining_weights()
        F1(0)
        FR1(0)
        FR1rope(0)
        FR1b(0)
        FR2(0)
        FR2act(0)
        P.tag = "KT(0)"
        KT(0)
        FR3(0)
        if nblk > 1:
            F1(1)
        FR2ev(0)
        for n in range(nblk):
            nx = n + 1 < nblk
            if n + 2 < nblk:
                XLOAD(n + 2)
            if nx:
                kbs, pml = SC(n, kvhs=(0,))
                FR1(n + 1, mid=lambda n=n: SC(n, kvhs=(1,)))
            else:
                kbs, pml = SC(n)
                if nblk > 1:
                    WARM(16)
            if nx:
                FR1rope(n + 1)
            PV(n, kbs, pml)
            PVb(n)
            SP(n)
            if n > 0:
                WOfin(n - 1)
            if nx:
                FR1b(n + 1)
            if n + 2 < nblk:
                S0a(n + 2)
            GT(n)
            YBT(n, with_k_of=(n + 1 if nx else None))
            GTact(n)
            PAB(n)
            if not nx and nblk > 1:
                WARM(18)
            if n + 2 < nblk:
                S0b(n + 2)
            if nx:
                FR2(n + 1)
                FR2act(n + 1)
            MT(n)
            if n + 2 < nblk:
                F1(n + 2)
            if nx:
                FR3(n + 1)
                FR2ev(n + 1)
            WO(n)
        WOfin(nblk - 1)

        fin_val = sem_o.count

        P.finalize(eng_sems)
        global LAST_PROG
        LAST_PROG = P

        with nc.Block() as block:
            @block.tensor
            def _(e):
                P.emit("pe", e)

            @block.scalar
            def _(e):
                P.emit("act", e)

            @block.vector
            def _(e):
                P.emit("dve", e)

            @block.gpsimd
            def _(e):
                P.emit("pool", e)

            @block.sync
            def _(e):
                P.emit("sp", e)
                e.wait_ge(sem_o.handle, fin_val)
    return nc


def _col_perm():
    def headperm(base):
        cols = []
        for g in range(4):
            for kvh in range(2):
                h = kvh * 4 + g
                cols.extend(range(base + h * 64, base + (h + 1) * 64))
        return cols
    perm = []
    perm += headperm(0)
    perm += list(range(512, 640))
    perm += list(range(640, 768))
    perm += headperm(768)
    perm += list(range(2304, 2816))
    perm += list(range(1792, 2304))
    perm += list(range(1280, 1792))
    perm += list(range(2816, 3840))
    perm += list(range(3840, 4864))
    return np.asarray(perm, dtype=np.int64)


_NC_CACHE = {}
LAST_PROG = None


def kernel(x, c, positions, w_ada, b_ada, g_pre, g_post, w_in, sinks,
           ln_v_g, ln_v_b, w_s, b_s, w_proj_a, w_proj_b, w_out, _nblk=NB):
    f32 = np.float32
    x = np.asarray(x, f32)
    B = x.shape[0]
    perm = _col_perm()
    w_in_p = np.ascontiguousarray(np.asarray(w_in, f32)[0][:, perm])
    rows = []
    for g in range(4):
        for kvh in range(2):
            h = kvh * 4 + g
            rows.extend(range(h * 64, (h + 1) * 64))
    w_pa_p = np.ascontiguousarray(np.asarray(w_proj_a, f32)[0][rows, :])
    w_s_t = np.ascontiguousarray(np.transpose(np.asarray(w_s, f32)[0], (2, 0, 1)))
    b_s_col = np.ascontiguousarray(np.asarray(b_s, f32)[0].T)
    ident = np.eye(128, dtype=f32)
    kk = np.arange(128)[:, None]
    qq = np.arange(128)[None, :]
    mprev = (qq < kk).astype(f32)
    mcur = (kk <= qq).astype(f32)
    masks = np.stack([mprev, mcur], axis=1)
    triu = np.ascontiguousarray(np.broadcast_to((kk <= qq).astype(f32)[:, None, :], (128, 4, 128)))
    invf = (10000.0 ** (-np.arange(32, dtype=f32) / f32(32))).astype(f32)

    shared = {
        "w_ada": np.ascontiguousarray(np.asarray(w_ada, f32)[0]),
        "b_ada": np.ascontiguousarray(np.asarray(b_ada, f32)[0]),
        "g_pre": np.ascontiguousarray(np.asarray(g_pre, f32)[0]),
        "g_post": np.ascontiguousarray(np.asarray(g_post, f32)[0]),
        "w_in_p": w_in_p,
        "sinks": np.ascontiguousarray(np.asarray(sinks, f32)[0]),
        "ln_v_g": np.ascontiguousarray(np.asarray(ln_v_g, f32)[0]),
        "ln_v_b": np.ascontiguousarray(np.asarray(ln_v_b, f32)[0]),
        "w_s_t": w_s_t,
        "b_s_col": b_s_col,
        "w_pa_p": w_pa_p,
        "w_proj_b": np.ascontiguousarray(np.asarray(w_proj_b, f32)[0]),
        "w_out": np.ascontiguousarray(np.asarray(w_out, f32)[0]),
        "ident": ident,
        "masks": np.ascontiguousarray(masks),
        "triu": triu,
        "invf": invf,
    }
    c = np.asarray(c, f32)
    positions = np.asarray(positions, np.int32)
    in_maps = []
    for b in range(B):
        m = dict(shared)
        m["x"] = np.ascontiguousarray(x[b])
        m["c_col"] = np.ascontiguousarray(c[b].reshape(8, 128).T)
        m["pos_col"] = np.ascontiguousarray(positions[b].reshape(NB, 128).T)
        in_maps.append(m)
    if _nblk not in _NC_CACHE:
        _NC_CACHE[_nblk] = build_program(_nblk)
    nc = _NC_CACHE[_nblk]
    res = run_bass_kernel_spmd(nc, in_maps, core_ids=list(range(B)))
    out = np.stack([np.asarray(res.results[b]["y"], f32) for b in range(B)], axis=0)
    return out
```

```python
import math
import sys
from contextlib import ExitStack

import numpy as np
import concourse.bass as bass
import concourse.mybir as mybir
from concourse.bass_utils import run_bass_kernel_spmd

F32 = mybir.dt.float32
BF16 = mybir.dt.bfloat16
I32 = mybir.dt.int32
AF = mybir.ActivationFunctionType
ALU = mybir.AluOpType

D = 1024
SEQ = 4096
NB = SEQ // 128
D_IN = 4864
EPS = 1e-6
TWO_PI = 2.0 * math.pi
CW1 = float(np.float32(6.28125))
CW2 = float(np.float32(TWO_PI - 6.28125))

SAME_ENGINE_STRICT = False

C_Q, C_K, C_V, C_ZA, C_ZB, C_VB, C_U, C_GA, C_GB = 0, 512, 640, 768, 1280, 1792, 2304, 2816, 3840


class Sem:
    def __init__(self, handle):
        self.handle = handle
        self.count = 0


class Buf:
    __slots__ = ("name", "w", "rs", "ro", "last_seq", "psum")

    def __init__(self, name, ro=False):
        self.name = name
        self.last_seq = -1
        self.psum = False
        self.w = None
        self.rs = {}
        self.ro = ro


class Op:
    __slots__ = ("eng", "idx", "fn", "deps", "dma", "sem", "val", "needed", "waits", "tag", "line")


class Prog:
    ENGS = ("pe", "act", "dve", "pool", "sp")

    def __init__(self):
        self.ops = {e: [] for e in self.ENGS}
        self.all_setup = []
        self.tag = "setup"
        self.seq = 0

    @staticmethod
    def _key(o):
        return ("d", id(o.sem)) if o.dma else ("e", o.eng)

    def op(self, eng, fn, reads=(), writes=(), dma_sem=None, after=()):
        o = Op()
        o.eng, o.fn, o.dma, o.needed, o.waits = eng, fn, dma_sem is not None, False, None
        o.idx = len(self.ops[eng])
        self.seq += 1
        for b in reads:
            b.last_seq = self.seq
        for b in writes:
            b.last_seq = self.seq
        o.tag = self.tag
        o.line = sys._getframe(1).f_lineno
        cand = []
        for b in reads:
            if b.w is not None:
                cand.append((b.w, True))
            if b.psum:
                for r in b.rs.values():
                    cand.append((r, False))
        for b in writes:
            for r in b.rs.values():
                cand.append((r, False))
            if b.w is not None:
                cand.append((b.w, False))
        for a in after:
            cand.append((a, True))
        best = {}
        for d, raw in cand:
            if d is o:
                continue
            if not d.dma and d.eng == eng and not o.dma:
                if eng == "pe" or (not raw and not SAME_ENGINE_STRICT):
                    continue
            k = self._key(d)
            cur = best.get(k)
            if cur is None or (d.dma and d.val > cur.val) or (not d.dma and d.idx > cur.idx):
                best[k] = d
        o.deps = list(best.values())
        for d in o.deps:
            d.needed = True
        if o.dma:
            dma_sem.count += 16
            o.sem, o.val = dma_sem, dma_sem.count
        else:
            o.sem, o.val = None, None
        for b in reads:
            if not b.ro:
                k = self._key(o)
                b.rs[k] = o
        for b in writes:
            b.w = o
            b.rs = {}
        self.ops[eng].append(o)
        return o

    def finalize(self, eng_sems):
        for e in self.ENGS:
            c = 0
            for o in self.ops[e]:
                if not o.dma:
                    o.sem = eng_sems[e]
                    if o.needed:
                        c += 1
                        o.val = c
            eng_sems[e].count = c
        for e in self.ENGS:
            waited = {}
            for o in self.ops[e]:
                req = {}
                for d in o.deps:
                    assert d.val is not None
                    k = id(d.sem)
                    if k not in req or req[k][1] < d.val:
                        req[k] = (d.sem, d.val)
                ws = []
                for k, (s, v) in req.items():
                    assert v <= s.count
                    if waited.get(k, 0) < v:
                        waited[k] = v
                        ws.append((s, v))
                o.waits = ws

    def emit(self, eng_name, e):
        for o in self.ops[eng_name]:
            for s, v in o.waits:
                e.wait_ge(s.handle, v)
            inst = o.fn(e)
            if o.dma:
                inst.then_inc(o.sem.handle, 16)
            elif o.needed:
                inst.then_inc(o.sem.handle, 1)


class T:
    def __init__(self, ap, name, ro=False, buf=None):
        self.ap = ap
        self.b = buf if buf is not None else Buf(name, ro)


def build_program(nblk=NB):
    nc = bass.Bass("TRN2", target_bir_lowering=False)
    P = Prog()

    def din(name, shape, dt=F32):
        return nc.dram_tensor(name, list(shape), dt, kind="ExternalInput").ap()

    x_d = din("x", [SEQ, D])
    y_d = nc.dram_tensor("y", [SEQ, D], F32, kind="ExternalOutput").ap()
    ccol_d = din("c_col", [128, 8])
    pos_d = din("pos_col", [128, NB], I32)
    wada_d = din("w_ada", [D, 3 * D])
    bada_d = din("b_ada", [3 * D])
    gpre_d = din("g_pre", [D])
    gpost_d = din("g_post", [D])
    win_d = din("w_in_p", [D, D_IN])
    sinks_d = din("sinks", [8])
    lng_d = din("ln_v_g", [512])
    lnb_d = din("ln_v_b", [512])
    wst_d = din("w_s_t", [128, 4, 128])
    bscol_d = din("b_s_col", [128, 4])
    wpa_d = din("w_pa_p", [512, D])
    wpb_d = din("w_proj_b", [512, D])
    wout_d = din("w_out", [D, D])
    ident_d = din("ident", [128, 128])
    masks_d = din("masks", [128, 2, 128])
    triu_d = din("triu", [128, 4, 128])
    invf_d = din("invf", [32])

    with ExitStack() as es:
        def sb(name, shape, dt):
            return es.enter_context(nc.sbuf_tensor(name, list(shape), dt))

        w_in_h = sb("w_in_sb", [128, 8, D_IN], BF16)
        w_pa_h = sb("w_pa_sb", [128, 4, D], BF16)
        w_pb_h = sb("w_pb_sb", [128, 4, D], BF16)
        w_out_h = sb("w_out_sb", [128, 8, D], BF16)
        gs_h = sb("gs_bc", [128, D], F32)
        sh_h = sb("shift_bc", [128, D], F32)
        gg_h = sb("gg_bc", [128, D], F32)
        ident_h = sb("ident_bf", [128, 128], BF16)
        masks_h = sb("masks_bf", [128, 2, 128], BF16)
        wT_h = sb("wT_bf", [128, 4, 128], BF16)
        bs_h = sb("bs_col", [128, 4], F32)
        esk_h = sb("esink_t", [128, 4, 128], F32)
        lg_h = sb("lg_bc", [128, 512], F32)
        lb_h = sb("lb_bc", [128, 512], F32)
        cos_h = sb("cosT", [128, NB, 32], F32)
        sin_h = sb("sinT", [128, NB, 32], F32)
        kT_h = sb("kT_ring", [128, 3, 128], BF16)
        va_h = sb("vaug_ring", [128, 3, 256], BF16)
        sm_h = sb("smalls", [128, 96], F32)
        nhalf_h = sb("nhalf", [128, 1], F32)
        posi_h = sb("posi", [128, NB], I32)

        SCR_WORDS = 17216
        scr_h = sb("scratch", [128, SCR_WORDS], F32)

        banks = []
        for i in range(8):
            h = es.enter_context(nc.psum_tensor("bank%d" % i, [128, 512], F32))
            banks.append(T(h[:, :], "bank%d" % i))
            banks[-1].b.psum = True
        bank_rr = [0]

        pinned = set()

        def next_bank():
            cands = [t for t in banks if id(t) not in pinned]
            assert cands, "all PSUM banks pinned"
            b = min(cands, key=lambda t: t.b.last_seq)
            P.seq += 1
            b.b.last_seq = P.seq
            return b

        def pin(*bs):
            for t in bs:
                pinned.add(id(t))

        def unpin(*bs):
            for t in bs:
                pinned.discard(id(t))

        eng_sems = {e: Sem(es.enter_context(nc.semaphore("s_" + e))) for e in ("pe", "act", "dve", "pool")}
        eng_sems["sp"] = Sem(es.enter_context(nc.semaphore("s_sp")))

        def dsem(name):
            return Sem(es.enter_context(nc.semaphore(name)))

        sem_small = dsem("d_small")
        sem_smB = dsem("d_smB")
        sem_smC = dsem("d_smC")
        sem_x = dsem("d_x")
        sem_o = dsem("d_o")
        sem_wada = [dsem("d_wada%d" % k) for k in range(8)]
        sem_win = [dsem("d_win%d" % k) for k in range(5)]
        sem_wp = [dsem("d_wp%d" % k) for k in range(3)]
        sem_bc = dsem("d_bc")
        sem_c = dsem("d_c")
        sem_xs = [dsem("d_xs0"), dsem("d_xs1")]
        sem_bada = [dsem("d_bada0"), dsem("d_bada1")]
        sem_bada2 = dsem("d_bada2")

        w_in = T(w_in_h[:, :, :], "w_in", ro=False)
        w_in_grp = [T(None, "w_in_g%d" % i, ro=True) for i in range(5)]
        w_pa = T(w_pa_h[:, :, :], "w_pa")
        w_pb = T(w_pb_h[:, :, :], "w_pb")
        w_out = T(w_out_h[:, :, :], "w_out")
        gs_bc = T(gs_h[:, :], "gs_bc")
        shift_bc = T(sh_h[:, :], "shift_bc")
        gg_bc = T(gg_h[:, :], "gg_bc")
        ident = T(ident_h[:, :], "ident", ro=True)
        masks = T(masks_h[:, :, :], "masks", ro=True)
        wT = T(wT_h[:, :, :], "wT")
        bs_col = T(bs_h[:, :], "bs_col", ro=True)
        esink_t = T(esk_h[:, :, :], "esink_t")
        lg_bc = T(lg_h[:, :], "lg_bc", ro=True)
        lb_bc = T(lb_h[:, :], "lb_bc", ro=True)
        cosT = T(cos_h[:, :, :], "cosT")
        sinT = T(sin_h[:, :, :], "sinT")
        kT_ring = [T(kT_h[:, s, :], "kT%d" % s) for s in range(3)]
        va_ring = [T(va_h[:, s, :], "va%d" % s) for s in range(3)]
        nhalf = T(nhalf_h[:, :], "nhalf")
        posi = T(posi_h[:, :], "posi", ro=True)

        sm_off = [0]

        def small(n, name):
            t = T(sm_h[:, sm_off[0]:sm_off[0] + n], name)
            sm_off[0] += n
            assert sm_off[0] <= 64
            return t

        c_col = small(8, "c_col")
        c_act = small(8, "c_act")
        sk_bc = small(8, "sk_bc")
        esk = small(8, "esk")
        ssq = small(1, "ssq")
        rs1 = small(1, "rs1")
        bnst = small(6, "bnst")
        bnmv = small(2, "bnmv")
        rs2 = small(1, "rs2")
        ssq2 = small(2, "ssq2")
        rs3 = small(1, "rs3")
        eps_col = small(1, "eps_col")
        gjunk = small(1, "gjunk")
        posf = T(sm_h[:, 64:96], "posf")
        assert sm_off[0] <= 64

        def carve(off_words, nwords, dt, name, buf=None):
            ap = scr_h[:, off_words:off_words + nwords]
            if dt is BF16:
                ap = ap.bitcast(BF16)
            return T(ap, name, buf=buf)

        wada = [carve(k * 1536, 1536, BF16, "wada%d" % k) for k in range(8)]
        bada_bcs = [carve(12288, 1024, F32, "bada_bc0"), carve(15904, 1024, F32, "bada_bc1")]
        gpre_bc = carve(13312, 1024, F32, "gpre_bc")
        gpost_bc = carve(14336, 1024, F32, "gpost_bc")
        crep = carve(15360, 512, BF16, "crep")
        invf_bc = carve(15872, 32, F32, "invf_bc")
        wst_f = carve(15904, 0, F32, "dummy")
        setup_tiles = wada + bada_bcs + [gpre_bc, gpost_bc, crep, invf_bc]

        wo_f32 = w_out_h[:, :, :].rearrange("p a b -> p (a b)").bitcast(F32)

        def ropescr(i):
            return T(wo_f32[:, i * 1024:(i + 1) * 1024], "ropescr%d" % i, buf=w_out.b)

        def dma(eng, out_t, out_ap, in_ap, sem, reads=()):
            return P.op(eng, lambda e: e.dma_start(out=out_ap, in_=in_ap),
                        reads=[r.b for r in reads], writes=[out_t.b], dma_sem=sem)

        small_tokens = []

        small_groups = {"A": (sem_small, []), "B": (sem_smB, []), "C": (sem_smC, [])}

        def dma_small(out_t, in_ap, eng="sp", out_ap=None, grp="B"):
            sem_, lst = small_groups[grp]
            o = dma(eng, out_t, out_t.ap if out_ap is None else out_ap, in_ap, sem_)
            lst.append(o)
            return o

        x_stage = [T(w_pa_h[:, :, :].rearrange("p a b -> p (a b)").bitcast(F32)[:, 0:1024], "xst0", buf=w_pa.b),
                   T(w_pb_h[:, :, :].rearrange("p a b -> p (a b)").bitcast(F32)[:, 0:1024], "xst1", buf=w_pb.b)]
        dma("sp", c_col, c_col.ap, ccol_d[:, :], sem_c)
        for i_ in range(min(2, nblk)):
            dma("sp", x_stage[i_], x_stage[i_].ap, x_d[i_ * 128:(i_ + 1) * 128, :], sem_xs[i_])
        dma_small(posi, pos_d[:, :], grp="A")
        dma_small(invf_bc, invf_d.partition_broadcast(128), grp="A")
        dma_small(sk_bc, sinks_d.partition_broadcast(128), grp="A")
        dma_small(bs_col, bscol_d[:, :])
        dma_small(lg_bc, lng_d.partition_broadcast(128))
        dma_small(lb_bc, lnb_d.partition_broadcast(128))
        dma_small(gpre_bc, gpre_d.partition_broadcast(128))
        dma_small(gpost_bc, gpost_d.partition_broadcast(128))
        for t3_ in range(2):
            dma("sp", bada_bcs[t3_], bada_bcs[t3_].ap, bada_d[t3_ * 1024:(t3_ + 1) * 1024].partition_broadcast(128),
                sem_bada[t3_])
        bada2_pre = T(w_pb_h[:, :, :].rearrange("p a b -> p (a b)").bitcast(F32)[:, 1024:2048], "bada2", buf=w_pb.b)
        dma("sp", bada2_pre, bada2_pre.ap, bada_d[2048:3072].partition_broadcast(128), sem_bada2)
        dma_small(ident, ident_d[:, :], eng="pool", grp="C")
        dma_small(masks, masks_d[:, :, :], eng="pool", grp="C")
        for sem_, lst in small_groups.values():
            for o in lst:
                o.val = sem_.count

        for k in range(8):
            dma("pool", wada[k], wada[k].ap, wada_d[k * 128:(k + 1) * 128, :], sem_wada[k])

        grp_cols = [(0, 768), (768, 1792), (1792, 2816), (2816, 3840), (3840, 4864)]

        def load_win_group(i, after=None):
            c0, c1 = grp_cols[i]
            src = win_d[:, c0:c1].rearrange("(k p) n -> p k n", p=128)
            return P.op("pool", lambda e: e.dma_start(out=w_in_h[:, :, c0:c1], in_=src),
                        reads=[], writes=[w_in_grp[i].b], dma_sem=sem_win[i],
                        after=([] if after is None else [after.b.w]))

        def wgrp(col):
            for i, (c0, c1) in enumerate(grp_cols):
                if c0 <= col < c1:
                    return w_in_grp[i]
            raise AssertionError

        load_win_group(0, after=wada[5])
        load_win_group(1, after=wada[5])
        load_win_group(2, after=wada[7])

        P.op("act", lambda e: e.activation(out=c_act.ap, in_=c_col.ap, func=AF.Silu),
             reads=[c_col.b], writes=[c_act.b])
        crep3 = crep.ap.rearrange("p (k m) -> p k m", k=8)
        P.op("dve", lambda e: e.tensor_copy(out=crep3, in_=c_act.ap.unsqueeze(2).to_broadcast([128, 8, 128])),
             reads=[c_act.b], writes=[crep.b])
        P.op("act", lambda e: e.activation(out=esk.ap, in_=sk_bc.ap, func=AF.Exp),
             reads=[sk_bc.b], writes=[esk.b])
        P.op("dve", lambda e: e.tensor_copy(out=esk_h[64:128, :, :],
                                            in_=sm_h[64:128, 24:28].unsqueeze(2).to_broadcast([64, 4, 128])),
             reads=[esk.b], writes=[esink_t.b])
        P.op("dve", lambda e: e.tensor_copy(out=esk_h[0:64, :, :],
                                            in_=sm_h[0:64, 28:32].unsqueeze(2).to_broadcast([64, 4, 128])),
             reads=[esk.b, esink_t.b], writes=[esink_t.b])
        assert esk.ap.shape == (128, 8)
        P.op("pool", lambda e: e.memset(nhalf.ap, -0.5), writes=[nhalf.b])
        P.op("pool", lambda e: e.memset(eps_col.ap, EPS), writes=[eps_col.b])
        P.op("pool", lambda e: e.memset(gjunk.ap, 0.0), writes=[gjunk.b])
        for s in range(3):
            P.op("pool", lambda e, s=s: e.memset(va_h[:, s, 64:192], 1.0), writes=[va_ring[s].b])

        ang, kf, rr, rc = ropescr(0), ropescr(1), ropescr(2), ropescr(3)
        ki_ap = kf.ap.bitcast(I32)
        wb = [w_out.b]
        P.op("dve", lambda e: e.tensor_copy(out=posf.ap, in_=posi.ap), reads=[posi.b], writes=[posf.b])
        P.op("dve", lambda e: e.tensor_tensor(
            out=ang.ap.rearrange("p (a b) -> p a b", a=NB),
            in0=posf.ap.unsqueeze(2).to_broadcast([128, NB, 32]),
            in1=invf_bc.ap.unsqueeze(1).to_broadcast([128, NB, 32]), op=ALU.mult),
            reads=[posf.b, invf_bc.b], writes=wb)
        P.op("dve", lambda e: e.tensor_scalar(out=rr.ap, in0=ang.ap, scalar1=1.0 / TWO_PI, scalar2=None, op0=ALU.mult),
             reads=wb, writes=wb)
        P.op("dve", lambda e: e.tensor_copy(out=ki_ap, in_=rr.ap), reads=wb, writes=wb)
        P.op("dve", lambda e: e.tensor_copy(out=rc.ap, in_=ki_ap), reads=wb, writes=wb)
        P.op("dve", lambda e: e.scalar_tensor_tensor(out=rr.ap, in0=rc.ap, scalar=-CW1, in1=ang.ap,
                                                     op0=ALU.mult, op1=ALU.add), reads=wb, writes=wb)
        P.op("dve", lambda e: e.scalar_tensor_tensor(out=rr.ap, in0=rc.ap, scalar=-CW2, in1=rr.ap,
                                                     op0=ALU.mult, op1=ALU.add), reads=wb, writes=wb)

        def wrap(t, tmp):
            P.op("dve", lambda e: e.tensor_scalar(out=tmp.ap, in0=t.ap, scalar1=math.pi, scalar2=-TWO_PI,
                                                  op0=ALU.is_gt, op1=ALU.mult), reads=wb, writes=wb)
            P.op("dve", lambda e: e.tensor_tensor(out=t.ap, in0=t.ap, in1=tmp.ap, op=ALU.add), reads=wb, writes=wb)
            P.op("dve", lambda e: e.tensor_scalar(out=tmp.ap, in0=t.ap, scalar1=-math.pi, scalar2=TWO_PI,
                                                  op0=ALU.is_lt, op1=ALU.mult), reads=wb, writes=wb)
            P.op("dve", lambda e: e.tensor_tensor(out=t.ap, in0=t.ap, in1=tmp.ap, op=ALU.add), reads=wb, writes=wb)
            P.op("dve", lambda e: e.tensor_scalar(out=t.ap, in0=t.ap, scalar1=-math.pi, scalar2=math.pi,
                                                  op0=ALU.max, op1=ALU.min), reads=wb, writes=wb)

        wrap(rr, rc)
        P.op("dve", lambda e: e.tensor_scalar(out=ang.ap, in0=rr.ap, scalar1=math.pi / 2, scalar2=None, op0=ALU.add),
             reads=wb, writes=wb)
        wrap(ang, rc)
        P.op("act", lambda e: e.activation(out=sin_h[:, :, :].rearrange("p a b -> p (a b)"), in_=rr.ap, func=AF.Sin),
             reads=wb, writes=[sinT.b])
        P.op("act", lambda e: e.activation(out=cos_h[:, :, :].rearrange("p a b -> p (a b)"), in_=ang.ap, func=AF.Sin),
             reads=wb, writes=[cosT.b])

        ada_banks = [next_bank() for _ in range(6)]
        for k in range(8):
            for nb in range(6):
                P.op("pe", lambda e, k=k, nb=nb: e.matmul(
                    ada_banks[nb].ap, crep3[:, k, :], wada[k].ap[:, nb * 512:(nb + 1) * 512],
                    start=(k == 0), stop=(k == 7)),
                    reads=[crep.b, wada[k].b], writes=[ada_banks[nb].b])
        bada_loads = {}
        bada2 = T(w_pb_h[:, :, :].rearrange("p a b -> p (a b)").bitcast(F32)[:, 1024:2048], "bada2", buf=w_pb.b)

        def load_bada(t3):
            bb = bada_bcs[t3 % 2]
            bada_loads[t3] = dma("sp", bb, bb.ap, bada_d[t3 * 1024:(t3 + 1) * 1024].partition_broadcast(128), sem_bada[t3 % 2])

        for t3 in (1, 0, 2):
            bada_bc = bada_bcs[t3 % 2] if t3 < 2 else bada2
            for h in range(2):
                bk = ada_banks[t3 * 2 + h]
                sl = slice(h * 512, (h + 1) * 512)
                if t3 == 0:
                    P.op("dve", lambda e, bk=bk, sl=sl, bb=bada_bc: e.tensor_tensor(out=sh_h[:, sl], in0=bk.ap, in1=bb.ap[:, sl],
                                                                          op=ALU.add),
                         reads=[bk.b, bada_bc.b], writes=[shift_bc.b])
                elif t3 == 1:
                    P.op("dve", lambda e, bk=bk, sl=sl, bb=bada_bc: e.scalar_tensor_tensor(
                        out=gs_h[:, sl], in0=bk.ap, scalar=1.0, in1=bb.ap[:, sl], op0=ALU.add, op1=ALU.add),
                        reads=[bk.b, bada_bc.b], writes=[gs_bc.b])
                    P.op("dve", lambda e, sl=sl: e.tensor_tensor(out=gs_h[:, sl], in0=gs_h[:, sl], in1=gpre_bc.ap[:, sl],
                                                                  op=ALU.mult),
                         reads=[gs_bc.b, gpre_bc.b], writes=[gs_bc.b])
                else:
                    P.op("dve", lambda e, bk=bk, sl=sl, bb=bada_bc: e.tensor_tensor(out=gg_h[:, sl], in0=bk.ap, in1=bb.ap[:, sl],
                                                                          op=ALU.add),
                         reads=[bk.b, bada_bc.b], writes=[gg_bc.b])
                    P.op("dve", lambda e, sl=sl: e.tensor_tensor(out=gg_h[:, sl], in0=gg_h[:, sl], in1=gpost_bc.ap[:, sl],
                                                                  op=ALU.mult),
                         reads=[gg_bc.b, gpost_bc.b], writes=[gg_bc.b])

        wsf, trf = ropescr(2), ropescr(3)
        o1_ = dma("sp", wsf, wsf.ap[:, 0:512], wst_d.rearrange("p a b -> p (a b)"), sem_bc)
        o2_ = dma("sp", trf, trf.ap[:, 0:512], triu_d.rearrange("p a b -> p (a b)"), sem_bc)
        P.op("dve", lambda e: e.tensor_tensor(out=wT_h[:, :, :].rearrange("p a b -> p (a b)"),
                                              in0=wsf.ap[:, 0:512], in1=trf.ap[:, 0:512], op=ALU.mult),
             reads=wb, writes=[wT.b])

        def load_remaining_weights():
            load_win_group(3)
            load_win_group(4)
            P.op("pool", lambda e: e.dma_start(out=w_pa_h[:, :, :], in_=wpa_d.rearrange("(k p) n -> p k n", p=128)),
                 writes=[w_pa.b], dma_sem=sem_wp[0])
            P.op("pool", lambda e: e.dma_start(out=w_pb_h[:, :, :], in_=wpb_d.rearrange("(k p) n -> p k n", p=128)),
                 writes=[w_pb.b], dma_sem=sem_wp[1])
            P.op("pool", lambda e: e.dma_start(out=w_out_h[:, :, :], in_=wout_d.rearrange("(k p) n -> p k n", p=128)),
                 writes=[w_out.b], dma_sem=sem_wp[2])
            w_out.b.ro = True
            w_pa.b.ro = True
            w_pb.b.ro = True


        alias_rs = {}
        alias_ws = []
        for t in setup_tiles:
            for k_, r in t.b.rs.items():
                alias_rs[(k_, id(t))] = r
            if t.b.w is not None:
                alias_ws.append(t.b.w)

        moff = [0]

        def mt(nwords, dt, name):
            nwords = (nwords + 7) // 8 * 8
            ap = scr_h[:, moff[0]:moff[0] + nwords]
            moff[0] += nwords
            assert moff[0] <= SCR_WORDS, (name, moff[0])
            if dt is BF16:
                ap = ap.bitcast(BF16)
            t = T(ap, name)
            i = 0
            for r in list(alias_rs.values()) + alias_ws:
                t.b.rs[("alias", i)] = r
                i += 1
            return t

        x_in = mt(1024, F32, "x_in")
        x_in_main = x_in
        hp = mt(1024, F32, "hp")
        h_bf = [mt(512, BF16, "h_bf%d" % i) for i in range(2)]
        hT = [mt(512, BF16, "hT%d" % i) for i in range(2)]
        tA = mt(512, F32, "tA")
        tB = mt(512, F32, "tB")
        tAk = mt(128, F32, "tAk")
        tBk = mt(128, F32, "tBk")
        qk_r = mt(320, BF16, "qk_r")
        qT = [mt(256, BF16, "qT%d" % i) for i in range(2)]
        sza = mt(256, BF16, "sza")
        szaT = [mt(256, BF16, "szaT%d" % i) for i in range(2)]
        Et = [mt(256, BF16, "E%d" % i) for i in range(4)]
        Pm = [mt(256, BF16, "Pm%d" % i) for i in range(4)]
        den = mt(512, F32, "den")
        ytmp = mt(512, F32, "ytmp")
        yaT = mt(256, BF16, "yaT")
        gv = mt(512, F32, "gv")
        vn_bf = [mt(256, BF16, "vn_bf%d" % i) for i in range(2)]
        su = [mt(256, BF16, "su%d" % i) for i in range(2)]
        szb = mt(256, BF16, "szb")
        sgz = mt(256, BF16, "sgz")
        sgz2 = mt(256, BF16, "sgz2")
        yb = mt(256, BF16, "yb")
        ybT = mt(256, BF16, "ybT")
        sig = mt(1024, BF16, "sig")
        m1 = mt(512, F32, "m1")
        m2 = mt(512, F32, "m2")
        merged = mt(512, BF16, "merged")
        mTt = mt(512, BF16, "mT")
        outt = mt(1024, F32, "out")

        w_in_ap = w_in_h

        def mm_group(outs, lhs_fn, K, reads_fn, mid=None):
            for k in range(K):
                if mid is not None and k == K // 2:
                    mid()
                for (bk, width, rhs_fn, extra) in outs:
                    P.op("pe", lambda e, k=k, bk=bk, width=width, rhs_fn=rhs_fn: e.matmul(
                        bk.ap[:, 0:width], lhs_fn(k), rhs_fn(k), start=(k == 0), stop=(k == K - 1)),
                        reads=reads_fn(k) + extra, writes=[bk.b])

        def XLOAD(n):
            P.tag = "XL(%d)" % n
            if n < 2:
                return
            dma("act", x_in, x_in.ap, x_d[n * 128:(n + 1) * 128, :], sem_x)

        def S0a(n):
            P.tag = "S0a(%d)" % n
            x_in = x_in_main if n >= 2 else x_stage[n]
            P.op("act", lambda e: e.activation(out=hp.ap, in_=x_in.ap, func=AF.Square, accum_out=ssq.ap),
                 reads=[x_in.b], writes=[hp.b, ssq.b])
            P.op("dve", lambda e: e.tensor_scalar(out=rs1.ap, in0=ssq.ap, scalar1=1.0 / D, scalar2=EPS,
                                                  op0=ALU.mult, op1=ALU.add), reads=[ssq.b], writes=[rs1.b])
            P.op("pool", lambda e: e.tensor_tensor(out=rs1.ap, in0=rs1.ap, in1=nhalf.ap, op=ALU.pow),
                 reads=[rs1.b, nhalf.b], writes=[rs1.b])

        def S0b(n):
            P.tag = "S0b(%d)" % n
            hb = h_bf[n % 2]
            x_in = x_in_main if n >= 2 else x_stage[n]
            P.op("dve", lambda e: e.scalar_tensor_tensor(out=hp.ap, in0=x_in.ap, scalar=rs1.ap, in1=gs_bc.ap,
                                                         op0=ALU.mult, op1=ALU.mult),
                 reads=[x_in.b, rs1.b, gs_bc.b], writes=[hp.b])
            P.op("dve", lambda e: e.tensor_tensor(out=hb.ap, in0=hp.ap, in1=shift_bc.ap, op=ALU.add),
                 reads=[hp.b, shift_bc.b], writes=[hb.b])

        def S0(n):
            S0a(n)
            S0b(n)

        def F1(n):
            P.tag = "F1(%d)" % n
            hb, ht = h_bf[n % 2], hT[n % 2]
            bk = next_bank()
            pb = bk.ap.bitcast(BF16)
            for k in range(8):
                P.op("pe", lambda e, k=k: e.transpose(pb[:, k * 128:(k + 1) * 128], hb.ap[:, k * 128:(k + 1) * 128], ident.ap),
                     reads=[hb.b, ident.b], writes=[bk.b])
            P.op("act", lambda e: e.activation(out=ht.ap, in_=pb, func=AF.Copy), reads=[bk.b], writes=[ht.b])

        def proj_banks(n, specs, mid=None):
            ht = hT[n % 2]
            ht3 = ht.ap.rearrange("p (k m) -> p k m", k=8)
            outs = []
            for (c0, width) in specs:
                bk = next_bank()
                outs.append((bk, width, (lambda k, c0=c0, width=width: w_in_ap[:, k, c0:c0 + width]), [wgrp(c0).b]))
            mm_group(outs, lambda k: ht3[:, k, :], 8, lambda k: [ht.b], mid=mid)
            return [o[0] for o in outs]

        def rope(src_bank, width, nh, dst_off, n, tA, tB):
            src = src_bank.ap[:, 0:width]
            s4 = src.rearrange("p (h t f) -> p h t f", h=nh, t=2, f=32)
            s3 = src.rearrange("p (h f) -> p h f", h=2 * nh, f=32)
            a3 = tA.ap[:, 0:width].rearrange("p (h f) -> p h f", h=2 * nh, f=32)
            b4 = tB.ap[:, 0:width].rearrange("p (h t f) -> p h t f", h=nh, t=2, f=32)
            cs = cos_h[:, n, :].unsqueeze(1).to_broadcast([128, 2 * nh, 32])
            sn = sin_h[:, n, :].unsqueeze(1).to_broadcast([128, nh, 32])
            P.op("dve", lambda e: e.tensor_tensor(out=a3, in0=s3, in1=cs, op=ALU.mult),
                 reads=[src_bank.b, cosT.b], writes=[tA.b])
            P.op("dve", lambda e: e.scalar_tensor_tensor(out=b4[:, :, 0, :], in0=s4[:, :, 1, :], scalar=-1.0, in1=sn,
                                                         op0=ALU.mult, op1=ALU.mult),
                 reads=[src_bank.b, sinT.b], writes=[tB.b])
            P.op("dve", lambda e: e.tensor_tensor(out=b4[:, :, 1, :], in0=s4[:, :, 0, :], in1=sn, op=ALU.mult),
                 reads=[src_bank.b, sinT.b], writes=[tB.b])
            P.op("pool", lambda e: e.tensor_tensor(out=qk_r.ap[:, dst_off:dst_off + width],
                                                   in0=tA.ap[:, 0:width],
                                                   in1=tB.ap[:, 0:width], op=ALU.add),
                 reads=[tA.b, tB.b], writes=[qk_r.b])

        za_bank = {}
        qk_bank = {}

        def FR1(n, mid=None):
            P.tag = "FR1(%d)" % n
            tag_ = P.tag

            def mid_():
                if mid is not None:
                    mid()
                    P.tag = tag_

            bq, bkv, bza = proj_banks(n, [(C_Q, 512), (C_K, 256), (C_ZA, 512)], mid=mid_)
            va = va_ring[n % 3]
            va4 = va.ap.rearrange("p (a b) -> p a b", a=4)
            P.op("act", lambda e: e.activation(out=va4[:, 0:4:3, :], in_=bkv.ap[:, 128:256].rearrange("p (a b) -> p a b", a=2),
                                               func=AF.Copy), reads=[bkv.b], writes=[va.b])
            za_bank[n] = bza
            qk_bank[n] = (bq, bkv)
            pin(bza, bq, bkv)

        def FR1rope(n):
            P.tag = "FR1rope(%d)" % n
            bq, bkv = qk_bank.pop(n)
            a3 = tA.ap[:, 0:512].rearrange("p (h f) -> p h f", h=16, f=32)
            a4 = tA.ap[:, 0:512].rearrange("p (h t f) -> p h t f", h=8, t=2, f=32)
            b4 = tB.ap[:, 0:512].rearrange("p (h t f) -> p h t f", h=8, t=2, f=32)
            o4 = qk_r.ap[:, 0:512].rearrange("p (h t f) -> p h t f", h=8, t=2, f=32)
            cs = cos_h[:, n, :].unsqueeze(1).to_broadcast([128, 16, 32])
            sn = sin_h[:, n, :].unsqueeze(1).to_broadcast([128, 8, 32])
            P.op("dve", lambda e: e.tensor_copy(out=tA.ap[:, 0:512], in_=bq.ap[:, 0:512]), reads=[bq.b], writes=[tA.b])
            rope(bkv, 128, 2, 512, n, tAk, tBk)
            P.op("pool", lambda e: e.tensor_tensor(out=b4[:, :, 0, :], in0=a4[:, :, 1, :], in1=sn, op=ALU.mult),
                 reads=[tA.b, sinT.b], writes=[tB.b])
            P.op("pool", lambda e: e.tensor_tensor(out=b4[:, :, 1, :], in0=a4[:, :, 0, :], in1=sn, op=ALU.mult),
                 reads=[tA.b, sinT.b], writes=[tB.b])
            P.op("pool", lambda e: e.tensor_tensor(out=a3, in0=a3, in1=cs, op=ALU.mult),
                 reads=[tA.b, cosT.b], writes=[tA.b])
            P.op("pool", lambda e: e.tensor_tensor(out=o4[:, :, 0, :], in0=a4[:, :, 0, :], in1=b4[:, :, 0, :], op=ALU.subtract),
                 reads=[tA.b, tB.b], writes=[qk_r.b])
            P.op("pool", lambda e: e.tensor_tensor(out=o4[:, :, 1, :], in0=a4[:, :, 1, :], in1=b4[:, :, 1, :], op=ALU.add),
                 reads=[tA.b, tB.b], writes=[qk_r.b])
            unpin(bq, bkv)

        def FR1b(n):
            P.tag = "FR1b(%d)" % n
            bza = za_bank.pop(n)
            P.op("act", lambda e: e.activation(out=sgz.ap, in_=bza.ap, func=AF.Sigmoid), reads=[bza.b], writes=[sgz.b])
            P.op("dve", lambda e: e.tensor_tensor(out=sza.ap, in0=bza.ap, in1=sgz.ap, op=ALU.mult),
                 reads=[bza.b, sgz.b], writes=[sza.b])
            unpin(bza)

        fr2_banks = {}

        def FR2(n):
            P.tag = "FR2(%d)" % n
            sl = n % 2
            fr2_banks[n] = proj_banks(n, [(C_ZB, 512), (C_VB, 512), (C_U, 512)])
            pin(*fr2_banks[n])

        def FR2act(n):
            P.tag = "FR2act(%d)" % n
            sl = n % 2
            bzb, bvb, bu = fr2_banks[n]
            P.op("act", lambda e: e.activation(out=sgz2.ap, in_=bzb.ap, func=AF.Sigmoid), reads=[bzb.b], writes=[sgz2.b])
            P.op("act", lambda e: e.activation(out=gv.ap, in_=bvb.ap, func=AF.Gelu), reads=[bvb.b], writes=[gv.b])
            P.op("act", lambda e: e.activation(out=su[sl].ap, in_=bu.ap, func=AF.Gelu), reads=[bu.b], writes=[su[sl].b])

        def FR2ev(n):
            P.tag = "FR2ev(%d)" % n
            sl = n % 2
            bzb, bvb, bu = fr2_banks.pop(n)
            unpin(bzb, bvb, bu)
            P.op("dve", lambda e: e.tensor_tensor(out=szb.ap, in0=bzb.ap, in1=sgz2.ap, op=ALU.mult),
                 reads=[bzb.b, sgz2.b], writes=[szb.b])
            P.op("dve", lambda e: e.bn_stats(out=bnst.ap, in_=gv.ap), reads=[gv.b], writes=[bnst.b])
            P.op("dve", lambda e: e.bn_aggr(out=bnmv.ap, in_=bnst.ap), reads=[bnst.b], writes=[bnmv.b])
            P.op("dve", lambda e: e.tensor_scalar(out=rs2.ap, in0=bnmv.ap[:, 1:2], scalar1=EPS, scalar2=None, op0=ALU.add),
                 reads=[bnmv.b], writes=[rs2.b])
            P.op("pool", lambda e: e.tensor_tensor(out=rs2.ap, in0=rs2.ap, in1=nhalf.ap, op=ALU.pow),
                 reads=[rs2.b, nhalf.b], writes=[rs2.b])
            P.op("dve", lambda e: e.tensor_scalar(out=gv.ap, in0=gv.ap, scalar1=bnmv.ap[:, 0:1], scalar2=rs2.ap,
                                                  op0=ALU.subtract, op1=ALU.mult),
                 reads=[gv.b, bnmv.b, rs2.b], writes=[gv.b])
            P.op("pool", lambda e: e.tensor_tensor(out=gv.ap, in0=gv.ap, in1=lg_bc.ap, op=ALU.mult),
                 reads=[gv.b, lg_bc.b], writes=[gv.b])
            P.op("pool", lambda e: e.tensor_tensor(out=vn_bf[sl].ap, in0=gv.ap, in1=lb_bc.ap, op=ALU.add),
                 reads=[gv.b, lb_bc.b], writes=[vn_bf[sl].b])
            P.op("pool", lambda e: e.tensor_tensor(out=su[sl].ap, in0=su[sl].ap, in1=szb.ap, op=ALU.mult),
                 reads=[su[sl].b, szb.b], writes=[su[sl].b])

        def FR3(n):
            P.tag = "FR3(%d)" % n
            sl = n % 2
            bt = next_bank()
            pbt = bt.ap.bitcast(BF16)
            for j in range(4):
                P.op("pe", lambda e, j=j: e.transpose(pbt[:, j * 128:(j + 1) * 128], qk_r.ap[:, j * 128:(j + 1) * 128], ident.ap),
                     reads=[qk_r.b, ident.b], writes=[bt.b])
            for j in range(4):
                P.op("pe", lambda e, j=j: e.transpose(pbt[:, 512 + j * 128:512 + (j + 1) * 128],
                                                      sza.ap[:, j * 128:(j + 1) * 128], ident.ap),
                     reads=[sza.b, ident.b], writes=[bt.b])
            P.op("dve", lambda e: e.tensor_copy(out=qT[sl].ap, in_=pbt[:, 0:512]), reads=[bt.b], writes=[qT[sl].b])
            P.op("act", lambda e: e.activation(out=szaT[sl].ap, in_=pbt[:, 512:1024], func=AF.Copy),
                 reads=[bt.b], writes=[szaT[sl].b])

        def KT(n, bt=None, col0=0):
            if bt is None:
                bt = next_bank()
            pbt = bt.ap.bitcast(BF16)
            P.op("pe", lambda e: e.transpose(pbt[:, col0:col0 + 128], qk_r.ap[:, 512:640], ident.ap),
                 reads=[qk_r.b, ident.b], writes=[bt.b])
            P.op("dve", lambda e: e.tensor_copy(out=kT_ring[n % 3].ap, in_=pbt[:, col0:col0 + 128]), reads=[bt.b],
                 writes=[kT_ring[n % 3].b])

        def SP(n):
            P.tag = "SP(%d)" % n
            sl = n % 2
            bsv = next_bank()
            for g in range(4):
                P.op("pe", lambda e, g=g: e.matmul(bsv.ap[:, g * 128:(g + 1) * 128], wT_h[:, g, :],
                                                   vn_bf[sl].ap[:, g * 128:(g + 1) * 128], start=True, stop=True),
                     reads=[wT.b, vn_bf[sl].b], writes=[bsv.b])
            for g in range(4):
                P.op("dve", lambda e, g=g: e.scalar_tensor_tensor(
                    out=yb.ap[:, g * 128:(g + 1) * 128], in0=bsv.ap[:, g * 128:(g + 1) * 128], scalar=bs_h[:, g:g + 1],
                    in1=su[sl].ap[:, g * 128:(g + 1) * 128], op0=ALU.add, op1=ALU.mult),
                    reads=[bsv.b, bs_col.b, su[sl].b], writes=[yb.b])

        sc_state = {}

        def SC(n, kvhs=(0, 1)):
            P.tag = "SC(%d)" % n
            sl = n % 2
            kbs = ([] if n == 0 else [(n - 1, 0)]) + [(n, 1)]
            q3 = qT[sl].ap
            st = sc_state.setdefault(n, {"pm": {}, "ei": 0})
            for kvh in kvhs:
                ps_ = slice(kvh * 64, (kvh + 1) * 64)
                for (kb, mi) in kbs:
                    bs_ = next_bank()
                    kt = kT_ring[kb % 3]
                    P.op("pe", lambda e, bs_=bs_, kt=kt, ps_=ps_: e.matmul(bs_.ap, kt.ap[ps_, :], q3[ps_, :], start=True, stop=True),
                         reads=[kt.b, qT[sl].b], writes=[bs_.b])
                    et = Et[st["ei"] % 4]
                    pm = Pm[kvh * 2 + mi]
                    st["ei"] += 1
                    P.op("act", lambda e, bs_=bs_, et=et: e.activation(out=et.ap, in_=bs_.ap, func=AF.Exp, scale=0.125),
                         reads=[bs_.b], writes=[et.b])
                    P.op("dve", lambda e, et=et, pm=pm, mi=mi: e.tensor_tensor(
                        out=pm.ap.rearrange("p (g q) -> p g q", g=4), in0=et.ap.rearrange("p (g q) -> p g q", g=4),
                        in1=masks_h[:, mi, :].unsqueeze(1).to_broadcast([128, 4, 128]), op=ALU.mult),
                         reads=[et.b, masks.b], writes=[pm.b])
                    st["pm"][(kvh, mi)] = (pm, kb)
            return kbs, st["pm"]

        gt_banks = {}

        def GT(n):
            P.tag = "GT(%d)" % n
            gt_banks[n] = proj_banks(n, [(C_GA, 512), (C_GA + 512, 512), (C_GB, 512), (C_GB + 512, 512)])
            pin(*gt_banks[n])

        def GTact(n, preload_exp=False):
            P.tag = "GTact(%d)" % n
            gbs_ = gt_banks.pop(n)
            unpin(*gbs_)
            for i, bk in enumerate(gbs_):
                off = i * 512
                P.op("act", lambda e, bk=bk, off=off: e.activation(out=sig.ap[:, off:off + 512], in_=bk.ap, func=AF.Sigmoid),
                     reads=[bk.b], writes=[sig.b])
            if preload_exp:
                P.op("act", lambda e: e.activation(out=gjunk.ap, in_=gjunk.ap, func=AF.Exp), reads=[], writes=[gjunk.b])

        pv_banks = {}

        def PV(n, kbs, pm_list):
            P.tag = "PV(%d)" % n
            sl = n % 2
            po = []
            for kvh in range(2):
                bo = next_bank()
                for i, (kb, mi) in enumerate(kbs):
                    pm, _ = pm_list[(kvh, mi)]
                    va = va_ring[kb % 3]
                    P.op("pe", lambda e, bo=bo, va=va, pm=pm, i=i, kvh=kvh: e.matmul(
                        bo.ap, va.ap[:, kvh * 128:(kvh + 1) * 128], pm.ap, start=(i == 0), stop=(i == len(kbs) - 1)),
                        reads=[va.b, pm.b], writes=[bo.b])
                po.append(bo)
            esk2 = esk_h[:, :, :].rearrange("p a b -> p (a b)")
            for kvh in range(2):
                s_rows = slice(64, 128) if kvh == 0 else slice(0, 64)
                bo = po[kvh]
                P.op("dve", lambda e, bo=bo, s_rows=s_rows: e.tensor_tensor(out=den.ap[s_rows, :], in0=bo.ap[s_rows, :],
                                                                              in1=esk2[s_rows, :], op=ALU.add),
                     reads=[bo.b, esink_t.b], writes=[den.b])
            P.op("act", lambda e: e.activation(out=den.ap, in_=den.ap, func=AF.Ln), reads=[den.b], writes=[den.b])
            P.op("act", lambda e: e.activation(out=den.ap, in_=den.ap, func=AF.Exp, scale=-1.0), reads=[den.b], writes=[den.b])
            pv_banks[n] = po
            pin(*po)

        def PVb(n):
            P.tag = "PVb(%d)" % n
            sl = n % 2
            po = pv_banks.pop(n)
            unpin(*po)
            for kvh in range(2):
                o_rows = slice(0, 64) if kvh == 0 else slice(64, 128)
                s_rows = slice(64, 128) if kvh == 0 else slice(0, 64)
                bo = po[kvh]
                P.op("dve", lambda e, bo=bo, s_rows=s_rows, o_rows=o_rows: e.tensor_tensor(
                    out=ytmp.ap[o_rows, :], in0=bo.ap[o_rows, :], in1=den.ap[s_rows, :], op=ALU.mult),
                    reads=[bo.b, den.b], writes=[ytmp.b])
            P.op("pool", lambda e: e.tensor_tensor(out=yaT.ap, in0=ytmp.ap, in1=szaT[sl].ap, op=ALU.mult),
                 reads=[ytmp.b, szaT[sl].b], writes=[yaT.b])

        def YBT(n, with_k_of=None):
            P.tag = "YBT(%d)" % n
            bt = next_bank()
            pbt = bt.ap.bitcast(BF16)
            for j in range(4):
                P.op("pe", lambda e, j=j: e.transpose(pbt[:, j * 128:(j + 1) * 128], yb.ap[:, j * 128:(j + 1) * 128], ident.ap),
                     reads=[yb.b, ident.b], writes=[bt.b])
            if with_k_of is not None:
                P.op("pe", lambda e: e.transpose(pbt[:, 512:640], qk_r.ap[:, 512:640], ident.ap),
                     reads=[qk_r.b, ident.b], writes=[bt.b])
            P.op("dve", lambda e: e.tensor_copy(out=ybT.ap, in_=pbt[:, 0:512]), reads=[bt.b], writes=[ybT.b])
            if with_k_of is not None:
                kt = kT_ring[with_k_of % 3]
                P.op("dve", lambda e: e.tensor_copy(out=kt.ap, in_=pbt[:, 512:640]), reads=[bt.b], writes=[kt.b])

        def PAB(n):
            P.tag = "PAB(%d)" % n
            ya3 = yaT.ap.rearrange("p (g m) -> p g m", g=4)
            yb3 = ybT.ap.rearrange("p (g m) -> p g m", g=4)
            pab = [next_bank(), next_bank()]
            mm_group([(pab[h], 512, (lambda k, h=h: w_pa_h[:, k, h * 512:(h + 1) * 512]), [w_pa.b]) for h in range(2)],
                     lambda k: ya3[:, k, :], 4, lambda k: [yaT.b])
            pbb = [next_bank(), next_bank()]
            mm_group([(pbb[h], 512, (lambda k, h=h: w_pb_h[:, k, h * 512:(h + 1) * 512]), [w_pb.b]) for h in range(2)],
                     lambda k: yb3[:, k, :], 4, lambda k: [ybT.b])
            for h in range(2):
                sl_ = slice(h * 512, (h + 1) * 512)
                mm = m1 if h == 0 else m2
                P.op("dve", lambda e, h=h, mm=mm, sl_=sl_: e.tensor_tensor(out=mm.ap, in0=pab[h].ap,
                                                                  in1=sig.ap[:, sl_], op=ALU.mult),
                     reads=[pab[h].b, sig.b], writes=[mm.b])
            for h in range(2):
                sl_ = slice(h * 512, (h + 1) * 512)
                mm = m1 if h == 0 else m2
                tmp_ = ytmp if h == 0 else den
                P.op("dve", lambda e, h=h, sl_=sl_, tmp_=tmp_: e.tensor_tensor(out=tmp_.ap, in0=pbb[h].ap,
                                                                    in1=sig.ap[:, 1024 + h * 512:1536 + h * 512], op=ALU.mult),
                     reads=[pbb[h].b, sig.b], writes=[tmp_.b])
                P.op("pool", lambda e, sl_=sl_, mm=mm, tmp_=tmp_: e.tensor_tensor(out=merged.ap[:, sl_], in0=tmp_.ap, in1=mm.ap,
                                                                               op=ALU.add),
                     reads=[tmp_.b, mm.b], writes=[merged.b])

        def MT(n):
            P.tag = "MT(%d)" % n
            dma("act", outt, outt.ap, x_d[n * 128:(n + 1) * 128, :], sem_o)
            bt = next_bank()
            pbt = bt.ap.bitcast(BF16)
            for j in range(8):
                P.op("pe", lambda e, j=j: e.transpose(pbt[:, j * 128:(j + 1) * 128], merged.ap[:, j * 128:(j + 1) * 128], ident.ap),
                     reads=[merged.b, ident.b], writes=[bt.b])
            P.op("dve", lambda e: e.tensor_copy(out=mTt.ap, in_=pbt), reads=[bt.b], writes=[mTt.b])

        def WO(n):
            P.tag = "WO(%d)" % n
            m3 = mTt.ap.rearrange("p (k m) -> p k m", k=8)
            pyb = [next_bank(), next_bank()]
            mm_group([(pyb[h], 512, (lambda k, h=h: w_out_h[:, k, h * 512:(h + 1) * 512]), [w_out.b]) for h in range(2)],
                     lambda k: m3[:, k, :], 8, lambda k: [mTt.b])
            for h in range(2):
                jk = sgz if h == 0 else sgz2
                P.op("act", lambda e, h=h, jk=jk: e.activation(out=jk.ap, in_=pyb[h].ap, func=AF.Square,
                                                               accum_out=ssq2.ap[:, h:h + 1]),
                     reads=[pyb[h].b], writes=[jk.b, ssq2.b])
            for h in range(2):
                sl_ = slice(h * 512, (h + 1) * 512)
                mm = m1 if h == 0 else m2
                P.op("dve", lambda e, h=h, sl_=sl_, mm=mm: e.tensor_tensor(out=mm.ap, in0=pyb[h].ap, in1=gg_h[:, sl_], op=ALU.mult),
                     reads=[pyb[h].b, gg_bc.b], writes=[mm.b])
            P.op("dve", lambda e: e.tensor_tensor(out=rs3.ap, in0=ssq2.ap[:, 0:1], in1=ssq2.ap[:, 1:2], op=ALU.add),
                 reads=[ssq2.b], writes=[rs3.b])

        def WOfin(n, split_store=False):
            P.tag = "WOfin(%d)" % n
            P.op("act", lambda e: e.activation(out=rs3.ap, in_=rs3.ap, func=AF.Ln, scale=1.0 / D, bias=eps_col.ap),
                 reads=[rs3.b, eps_col.b], writes=[rs3.b])
            P.op("act", lambda e: e.activation(out=rs3.ap, in_=rs3.ap, func=AF.Exp, scale=-0.5),
                 reads=[rs3.b], writes=[rs3.b])
            for h in range(2):
                sl_ = slice(h * 512, (h + 1) * 512)
                mm = m1 if h == 0 else m2
                P.op("dve", lambda e, sl_=sl_, mm=mm: e.scalar_tensor_tensor(
                    out=outt.ap[:, sl_], in0=mm.ap, scalar=rs3.ap, in1=outt.ap[:, sl_], op0=ALU.mult, op1=ALU.add),
                    reads=[mm.b, rs3.b, outt.b], writes=[outt.b])
                if split_store:
                    P.op("sp", lambda e, sl_=sl_: e.dma_start(out=y_d[n * 128:(n + 1) * 128, sl_], in_=outt.ap[:, sl_]),
                         reads=[outt.b], writes=[], dma_sem=sem_o)
            if split_store:
                return None
            return P.op("sp", lambda e: e.dma_start(out=y_d[n * 128:(n + 1) * 128, :], in_=outt.ap),
                        reads=[outt.b], writes=[], dma_sem=sem_o)

        def WARM(count):
            P.tag = "WARM"
            bk = next_bank()
            for _ in range(count):
                P.op("pe", lambda e: e.matmul(bk.ap, ident.ap, w_in_h[:, 0, 0:512], start=True, stop=True),
                     reads=[ident.b, w_in_grp[0].b], writes=[bk.b])

        XLOAD(0)
        S0(0)
        if nblk > 1:
            XLOAD(1)
            S0(1)
        load_remaining_weights()
        F1(0)
        FR1(0)
        FR1rope(0)
        FR1b(0)
        FR2(0)
        FR2act(0)
        P.tag = "KT(0)"
        KT(0)
        FR3(0)
        if nblk > 1:
            F1(1)
        FR2ev(0)
        for n in range(nblk):
            nx = n + 1 < nblk
            if n + 2 < nblk:
                XLOAD(n + 2)
            if nx:
                kbs, pml = SC(n, kvhs=(0,))
                FR1(n + 1, mid=lambda n=n: SC(n, kvhs=(1,)))
            else:
                kbs, pml = SC(n)
                if nblk > 1:
                    WARM(16)
            if nx:
                FR1rope(n + 1)
            PV(n, kbs, pml)
            PVb(n)
            SP(n)
            if n > 0:
                WOfin(n - 1)
            if nx:
                FR1b(n + 1)
            if n + 2 < nblk:
                S0a(n + 2)
            GT(n)
            YBT(n, with_k_of=(n + 1 if nx else None))
            GTact(n, preload_exp=(not nx and nblk > 1))
            PAB(n)
            if not nx and nblk > 1:
                WARM(18)
            if n + 2 < nblk:
                S0b(n + 2)
            if nx:
                FR2(n + 1)
                FR2act(n + 1)
            MT(n)
            if n + 2 < nblk:
                F1(n + 2)
            if nx:
                FR3(n + 1)
                FR2ev(n + 1)
            WO(n)
        WOfin(nblk - 1, split_store=True)

        fin_val = sem_o.count

        P.finalize(eng_sems)
        global LAST_PROG
        LAST_PROG = P

        with nc.Block() as block:
            @block.tensor
            def _(e):
                P.emit("pe", e)

            @block.scalar
            def _(e):
                P.emit("act", e)

            @block.vector
            def _(e):
                P.emit("dve", e)

            @block.gpsimd
            def _(e):
                P.emit("pool", e)

            @block.sync
            def _(e):
                P.emit("sp", e)
                e.wait_ge(sem_o.handle, fin_val)
    return nc


def _col_perm():
    def headperm(base):
        cols = []
        for g in range(4):
            for kvh in range(2):
                h = kvh * 4 + g
                cols.extend(range(base + h * 64, base + (h + 1) * 64))
        return cols
    perm = []
    perm += headperm(0)
    perm += list(range(512, 640))
    perm += list(range(640, 768))
    perm += headperm(768)
    perm += list(range(2304, 2816))
    perm += list(range(1792, 2304))
    perm += list(range(1280, 1792))
    perm += list(range(2816, 3840))
    perm += list(range(3840, 4864))
    return np.asarray(perm, dtype=np.int64)


_NC_CACHE = {}
LAST_PROG = None


def kernel(x, c, positions, w_ada, b_ada, g_pre, g_post, w_in, sinks,
           ln_v_g, ln_v_b, w_s, b_s, w_proj_a, w_proj_b, w_out, _nblk=NB):
    f32 = np.float32
    x = np.asarray(x, f32)
    B = x.shape[0]
    perm = _col_perm()
    w_in_p = np.ascontiguousarray(np.asarray(w_in, f32)[0][:, perm])
    rows = []
    for g in range(4):
        for kvh in range(2):
            h = kvh * 4 + g
            rows.extend(range(h * 64, (h + 1) * 64))
    w_pa_p = np.ascontiguousarray(np.asarray(w_proj_a, f32)[0][rows, :])
    w_s_t = np.ascontiguousarray(np.transpose(np.asarray(w_s, f32)[0], (2, 0, 1)))
    b_s_col = np.ascontiguousarray(np.asarray(b_s, f32)[0].T)
    ident = np.eye(128, dtype=f32)
    kk = np.arange(128)[:, None]
    qq = np.arange(128)[None, :]
    mprev = (qq < kk).astype(f32)
    mcur = (kk <= qq).astype(f32)
    masks = np.stack([mprev, mcur], axis=1)
    triu = np.ascontiguousarray(np.broadcast_to((kk <= qq).astype(f32)[:, None, :], (128, 4, 128)))
    invf = (10000.0 ** (-np.arange(32, dtype=f32) / f32(32))).astype(f32)

    shared = {
        "w_ada": np.ascontiguousarray(np.asarray(w_ada, f32)[0]),
        "b_ada": np.ascontiguousarray(np.asarray(b_ada, f32)[0]),
        "g_pre": np.ascontiguousarray(np.asarray(g_pre, f32)[0]),
        "g_post": np.ascontiguousarray(np.asarray(g_post, f32)[0]),
        "w_in_p": w_in_p,
        "sinks": np.ascontiguousarray(np.asarray(sinks, f32)[0]),
        "ln_v_g": np.ascontiguousarray(np.asarray(ln_v_g, f32)[0]),
        "ln_v_b": np.ascontiguousarray(np.asarray(ln_v_b, f32)[0]),
        "w_s_t": w_s_t,
        "b_s_col": b_s_col,
        "w_pa_p": w_pa_p,
        "w_proj_b": np.ascontiguousarray(np.asarray(w_proj_b, f32)[0]),
        "w_out": np.ascontiguousarray(np.asarray(w_out, f32)[0]),
        "ident": ident,
        "masks": np.ascontiguousarray(masks),
        "triu": triu,
        "invf": invf,
    }
    c = np.asarray(c, f32)
    positions = np.asarray(positions, np.int32)
    in_maps = []
    for b in range(B):
        m = dict(shared)
        m["x"] = np.ascontiguousarray(x[b])
        m["c_col"] = np.ascontiguousarray(c[b].reshape(8, 128).T)
        m["pos_col"] = np.ascontiguousarray(positions[b].reshape(NB, 128).T)
        in_maps.append(m)
    if _nblk not in _NC_CACHE:
        _NC_CACHE[_nblk] = build_program(_nblk)
    nc = _NC_CACHE[_nblk]
    res = run_bass_kernel_spmd(nc, in_maps, core_ids=list(range(B)))
    out = np.stack([np.asarray(res.results[b]["y"], f32) for b in range(B)], axis=0)
    return out
```

```python
import math
import sys
from contextlib import ExitStack

import numpy as np
import concourse.bass as bass
import concourse.mybir as mybir
from concourse.bass_utils import run_bass_kernel_spmd

F32 = mybir.dt.float32
BF16 = mybir.dt.bfloat16
I32 = mybir.dt.int32
AF = mybir.ActivationFunctionType
ALU = mybir.AluOpType

D = 1024
SEQ = 4096
NB = SEQ // 128
D_IN = 4864
EPS = 1e-6
TWO_PI = 2.0 * math.pi
CW1 = float(np.float32(6.28125))
CW2 = float(np.float32(TWO_PI - 6.28125))

SAME_ENGINE_STRICT = False

C_Q, C_K, C_V, C_ZA, C_ZB, C_VB, C_U, C_GA, C_GB = 0, 512, 640, 768, 1280, 1792, 2304, 2816, 3840


class Sem:
    def __init__(self, handle):
        self.handle = handle
        self.count = 0


class Buf:
    __slots__ = ("name", "w", "rs", "ro", "last_seq", "psum")

    def __init__(self, name, ro=False):
        self.name = name
        self.last_seq = -1
        self.psum = False
        self.w = None
        self.rs = {}
        self.ro = ro


class Op:
    __slots__ = ("eng", "idx", "fn", "deps", "dma", "sem", "val", "needed", "waits", "tag", "line")


class Prog:
    ENGS = ("pe", "act", "dve", "pool", "sp")

    def __init__(self):
        self.ops = {e: [] for e in self.ENGS}
        self.all_setup = []
        self.tag = "setup"
        self.seq = 0

    @staticmethod
    def _key(o):
        return ("d", id(o.sem)) if o.dma else ("e", o.eng)

    def op(self, eng, fn, reads=(), writes=(), dma_sem=None, after=()):
        o = Op()
        o.eng, o.fn, o.dma, o.needed, o.waits = eng, fn, dma_sem is not None, False, None
        o.idx = len(self.ops[eng])
        self.seq += 1
        for b in reads:
            b.last_seq = self.seq
        for b in writes:
            b.last_seq = self.seq
        o.tag = self.tag
        o.line = sys._getframe(1).f_lineno
        cand = []
        for b in reads:
            if b.w is not None:
                cand.append((b.w, True))
            if b.psum:
                for r in b.rs.values():
                    cand.append((r, False))
        for b in writes:
            for r in b.rs.values():
                cand.append((r, False))
            if b.w is not None:
                cand.append((b.w, False))
        for a in after:
            cand.append((a, True))
        best = {}
        for d, raw in cand:
            if d is o:
                continue
            if not d.dma and d.eng == eng and not o.dma:
                if eng == "pe" or (not raw and not SAME_ENGINE_STRICT):
                    continue
            k = self._key(d)
            cur = best.get(k)
            if cur is None or (d.dma and d.val > cur.val) or (not d.dma and d.idx > cur.idx):
                best[k] = d
        o.deps = list(best.values())
        for d in o.deps:
            d.needed = True
        if o.dma:
            dma_sem.count += 16
            o.sem, o.val = dma_sem, dma_sem.count
        else:
            o.sem, o.val = None, None
        for b in reads:
            if not b.ro:
                k = self._key(o)
                b.rs[k] = o
        for b in writes:
            b.w = o
            b.rs = {}
        self.ops[eng].append(o)
        return o

    def finalize(self, eng_sems):
        for e in self.ENGS:
            c = 0
            for o in self.ops[e]:
                if not o.dma:
                    o.sem = eng_sems[e]
                    if o.needed:
                        c += 1
                        o.val = c
            eng_sems[e].count = c
        for e in self.ENGS:
            waited = {}
            for o in self.ops[e]:
                req = {}
                for d in o.deps:
                    assert d.val is not None
                    k = id(d.sem)
                    if k not in req or req[k][1] < d.val:
                        req[k] = (d.sem, d.val)
                ws = []
                for k, (s, v) in req.items():
                    assert v <= s.count
                    if waited.get(k, 0) < v:
                        waited[k] = v
                        ws.append((s, v))
                o.waits = ws

    def emit(self, eng_name, e):
        for o in self.ops[eng_name]:
            for s, v in o.waits:
                e.wait_ge(s.handle, v)
            inst = o.fn(e)
            if o.dma:
                inst.then_inc(o.sem.handle, 16)
            elif o.needed:
                inst.then_inc(o.sem.handle, 1)


class T:
    def __init__(self, ap, name, ro=False, buf=None):
        self.ap = ap
        self.b = buf if buf is not None else Buf(name, ro)


def build_program(nblk=NB):
    nc = bass.Bass("TRN2", target_bir_lowering=False)
    P = Prog()

    def din(name, shape, dt=F32):
        return nc.dram_tensor(name, list(shape), dt, kind="ExternalInput").ap()

    x_d = din("x", [SEQ, D])
    y_d = nc.dram_tensor("y", [SEQ, D], F32, kind="ExternalOutput").ap()
    ccol_d = din("c_col", [128, 8])
    pos_d = din("pos_col", [128, NB], I32)
    wada_d = din("w_ada", [D, 3 * D])
    bada_d = din("b_ada", [3 * D])
    gpre_d = din("g_pre", [D])
    gpost_d = din("g_post", [D])
    win_d = din("w_in_p", [D, D_IN])
    sinks_d = din("sinks", [8])
    lng_d = din("ln_v_g", [512])
    lnb_d = din("ln_v_b", [512])
    wst_d = din("w_s_t", [128, 4, 128])
    bscol_d = din("b_s_col", [128, 4])
    wpa_d = din("w_pa_p", [512, D])
    wpb_d = din("w_proj_b", [512, D])
    wout_d = din("w_out", [D, D])
    ident_d = din("ident", [128, 128])
    masks_d = din("masks", [128, 2, 128])
    triu_d = din("triu", [128, 4, 128])
    invf_d = din("invf", [32])

    with ExitStack() as es:
        def sb(name, shape, dt):
            return es.enter_context(nc.sbuf_tensor(name, list(shape), dt))

        w_in_h = sb("w_in_sb", [128, 8, D_IN], BF16)
        w_pa_h = sb("w_pa_sb", [128, 4, D], BF16)
        w_pb_h = sb("w_pb_sb", [128, 4, D], BF16)
        w_out_h = sb("w_out_sb", [128, 8, D], BF16)
        gs_h = sb("gs_bc", [128, D], F32)
        sh_h = sb("shift_bc", [128, D], F32)
        gg_h = sb("gg_bc", [128, D], F32)
        ident_h = sb("ident_bf", [128, 128], BF16)
        masks_h = sb("masks_bf", [128, 2, 128], BF16)
        wT_h = sb("wT_bf", [128, 4, 128], BF16)
        bs_h = sb("bs_col", [128, 4], F32)
        esk_h = sb("esink_t", [128, 4, 128], F32)
        lg_h = sb("lg_bc", [128, 512], F32)
        lb_h = sb("lb_bc", [128, 512], F32)
        cos_h = sb("cosT", [128, NB, 32], F32)
        sin_h = sb("sinT", [128, NB, 32], F32)
        kT_h = sb("kT_ring", [128, 3, 128], BF16)
        va_h = sb("vaug_ring", [128, 3, 256], BF16)
        sm_h = sb("smalls", [128, 96], F32)
        nhalf_h = sb("nhalf", [128, 1], F32)
        posi_h = sb("posi", [128, NB], I32)

        SCR_WORDS = 17216
        scr_h = sb("scratch", [128, SCR_WORDS], F32)

        banks = []
        for i in range(8):
            h = es.enter_context(nc.psum_tensor("bank%d" % i, [128, 512], F32))
            banks.append(T(h[:, :], "bank%d" % i))
            banks[-1].b.psum = True
        bank_rr = [0]

        pinned = set()

        def next_bank():
            cands = [t for t in banks if id(t) not in pinned]
            assert cands, "all PSUM banks pinned"
            b = min(cands, key=lambda t: t.b.last_seq)
            P.seq += 1
            b.b.last_seq = P.seq
            return b

        def pin(*bs):
            for t in bs:
                pinned.add(id(t))

        def unpin(*bs):
            for t in bs:
                pinned.discard(id(t))

        eng_sems = {e: Sem(es.enter_context(nc.semaphore("s_" + e))) for e in ("pe", "act", "dve", "pool")}
        eng_sems["sp"] = Sem(es.enter_context(nc.semaphore("s_sp")))

        def dsem(name):
            return Sem(es.enter_context(nc.semaphore(name)))

        sem_small = dsem("d_small")
        sem_smB = dsem("d_smB")
        sem_smC = dsem("d_smC")
        sem_x = dsem("d_x")
        sem_o = dsem("d_o")
        sem_wada = [dsem("d_wada%d" % k) for k in range(8)]
        sem_win = [dsem("d_win%d" % k) for k in range(5)]
        sem_wp = [dsem("d_wp%d" % k) for k in range(3)]
        sem_bc = dsem("d_bc")
        sem_c = dsem("d_c")
        sem_xs = [dsem("d_xs0"), dsem("d_xs1")]
        sem_bada = [dsem("d_bada0"), dsem("d_bada1")]
        sem_bada2 = dsem("d_bada2")

        w_in = T(w_in_h[:, :, :], "w_in", ro=False)
        w_in_grp = [T(None, "w_in_g%d" % i, ro=True) for i in range(5)]
        w_pa = T(w_pa_h[:, :, :], "w_pa")
        w_pb = T(w_pb_h[:, :, :], "w_pb")
        w_out = T(w_out_h[:, :, :], "w_out")
        gs_bc = T(gs_h[:, :], "gs_bc")
        shift_bc = T(sh_h[:, :], "shift_bc")
        gg_bc = T(gg_h[:, :], "gg_bc")
        ident = T(ident_h[:, :], "ident", ro=True)
        masks = T(masks_h[:, :, :], "masks", ro=True)
        wT = T(wT_h[:, :, :], "wT")
        bs_col = T(bs_h[:, :], "bs_col", ro=True)
        esink_t = T(esk_h[:, :, :], "esink_t")
        lg_bc = T(lg_h[:, :], "lg_bc", ro=True)
        lb_bc = T(lb_h[:, :], "lb_bc", ro=True)
        cosT = T(cos_h[:, :, :], "cosT")
        sinT = T(sin_h[:, :, :], "sinT")
        kT_ring = [T(kT_h[:, s, :], "kT%d" % s) for s in range(3)]
        va_ring = [T(va_h[:, s, :], "va%d" % s) for s in range(3)]
        nhalf = T(nhalf_h[:, :], "nhalf")
        posi = T(posi_h[:, :], "posi", ro=True)

        sm_off = [0]

        def small(n, name):
            t = T(sm_h[:, sm_off[0]:sm_off[0] + n], name)
            sm_off[0] += n
            assert sm_off[0] <= 64
            return t

        c_col = small(8, "c_col")
        c_act = small(8, "c_act")
        sk_bc = small(8, "sk_bc")
        esk = small(8, "esk")
        ssq = small(1, "ssq")
        rs1 = small(1, "rs1")
        bnst = small(6, "bnst")
        bnmv = small(2, "bnmv")
        rs2 = small(1, "rs2")
        ssq2 = small(2, "ssq2")
        rs3 = small(1, "rs3")
        eps_col = small(1, "eps_col")
        gjunk = small(1, "gjunk")
        posf = T(sm_h[:, 64:96], "posf")
        assert sm_off[0] <= 64

        def carve(off_words, nwords, dt, name, buf=None):
            ap = scr_h[:, off_words:off_words + nwords]
            if dt is BF16:
                ap = ap.bitcast(BF16)
            return T(ap, name, buf=buf)

        wada = [carve(k * 1536, 1536, BF16, "wada%d" % k) for k in range(8)]
        bada_bcs = [carve(12288, 1024, F32, "bada_bc0"), carve(15904, 1024, F32, "bada_bc1")]
        gpre_bc = carve(13312, 1024, F32, "gpre_bc")
        gpost_bc = carve(14336, 1024, F32, "gpost_bc")
        crep = carve(15360, 512, BF16, "crep")
        invf_bc = carve(15872, 32, F32, "invf_bc")
        wst_f = carve(15904, 0, F32, "dummy")
        setup_tiles = wada + bada_bcs + [gpre_bc, gpost_bc, crep, invf_bc]

        wo_f32 = w_out_h[:, :, :].rearrange("p a b -> p (a b)").bitcast(F32)

        def ropescr(i):
            return T(wo_f32[:, i * 1024:(i + 1) * 1024], "ropescr%d" % i, buf=w_out.b)

        def dma(eng, out_t, out_ap, in_ap, sem, reads=()):
            return P.op(eng, lambda e: e.dma_start(out=out_ap, in_=in_ap),
                        reads=[r.b for r in reads], writes=[out_t.b], dma_sem=sem)

        small_tokens = []

        small_groups = {"A": (sem_small, []), "B": (sem_smB, []), "C": (sem_smC, [])}

        def dma_small(out_t, in_ap, eng="sp", out_ap=None, grp="B"):
            sem_, lst = small_groups[grp]
            o = dma(eng, out_t, out_t.ap if out_ap is None else out_ap, in_ap, sem_)
            lst.append(o)
            return o

        x_stage = [T(w_pa_h[:, :, :].rearrange("p a b -> p (a b)").bitcast(F32)[:, 0:1024], "xst0", buf=w_pa.b),
                   T(w_pb_h[:, :, :].rearrange("p a b -> p (a b)").bitcast(F32)[:, 0:1024], "xst1", buf=w_pb.b)]
        dma("sp", c_col, c_col.ap, ccol_d[:, :], sem_c)
        for i_ in range(min(2, nblk)):
            dma("sp", x_stage[i_], x_stage[i_].ap, x_d[i_ * 128:(i_ + 1) * 128, :], sem_xs[i_])
        dma_small(posi, pos_d[:, :], grp="A")
        dma_small(invf_bc, invf_d.partition_broadcast(128), grp="A")
        dma_small(sk_bc, sinks_d.partition_broadcast(128), grp="A")
        dma_small(bs_col, bscol_d[:, :])
        dma_small(lg_bc, lng_d.partition_broadcast(128))
        dma_small(lb_bc, lnb_d.partition_broadcast(128))
        dma_small(gpre_bc, gpre_d.partition_broadcast(128))
        dma_small(gpost_bc, gpost_d.partition_broadcast(128))
        for t3_ in range(2):
            dma("sp", bada_bcs[t3_], bada_bcs[t3_].ap, bada_d[t3_ * 1024:(t3_ + 1) * 1024].partition_broadcast(128),
                sem_bada[t3_])
        bada2_pre = T(w_pb_h[:, :, :].rearrange("p a b -> p (a b)").bitcast(F32)[:, 1024:2048], "bada2", buf=w_pb.b)
        dma("sp", bada2_pre, bada2_pre.ap, bada_d[2048:3072].partition_broadcast(128), sem_bada2)
        dma_small(ident, ident_d[:, :], eng="pool", grp="C")
        dma_small(masks, masks_d[:, :, :], eng="pool", grp="C")
        for sem_, lst in small_groups.values():
            for o in lst:
                o.val = sem_.count

        for k in range(8):
            dma("pool", wada[k], wada[k].ap, wada_d[k * 128:(k + 1) * 128, :], sem_wada[k])

        grp_cols = [(0, 768), (768, 1792), (1792, 2816), (2816, 3840), (3840, 4864)]

        def load_win_group(i, after=None):
            c0, c1 = grp_cols[i]
            src = win_d[:, c0:c1].rearrange("(k p) n -> p k n", p=128)
            return P.op("pool", lambda e: e.dma_start(out=w_in_h[:, :, c0:c1], in_=src),
                        reads=[], writes=[w_in_grp[i].b], dma_sem=sem_win[i],
                        after=([] if after is None else [after.b.w]))

        def wgrp(col):
            for i, (c0, c1) in enumerate(grp_cols):
                if c0 <= col < c1:
                    return w_in_grp[i]
            raise AssertionError

        load_win_group(0, after=wada[5])
        load_win_group(1, after=wada[5])
        load_win_group(2, after=wada[7])

        P.op("act", lambda e: e.activation(out=c_act.ap, in_=c_col.ap, func=AF.Silu),
             reads=[c_col.b], writes=[c_act.b])
        crep3 = crep.ap.rearrange("p (k m) -> p k m", k=8)
        P.op("dve", lambda e: e.tensor_copy(out=crep3, in_=c_act.ap.unsqueeze(2).to_broadcast([128, 8, 128])),
             reads=[c_act.b], writes=[crep.b])
        P.op("act", lambda e: e.activation(out=esk.ap, in_=sk_bc.ap, func=AF.Exp),
             reads=[sk_bc.b], writes=[esk.b])
        P.op("dve", lambda e: e.tensor_copy(out=esk_h[64:128, :, :],
                                            in_=sm_h[64:128, 24:28].unsqueeze(2).to_broadcast([64, 4, 128])),
             reads=[esk.b], writes=[esink_t.b])
        P.op("dve", lambda e: e.tensor_copy(out=esk_h[0:64, :, :],
                                            in_=sm_h[0:64, 28:32].unsqueeze(2).to_broadcast([64, 4, 128])),
             reads=[esk.b, esink_t.b], writes=[esink_t.b])
        assert esk.ap.shape == (128, 8)
        P.op("pool", lambda e: e.memset(nhalf.ap, -0.5), writes=[nhalf.b])
        P.op("pool", lambda e: e.memset(eps_col.ap, EPS), writes=[eps_col.b])
        P.op("pool", lambda e: e.memset(gjunk.ap, 0.0), writes=[gjunk.b])
        for s in range(3):
            P.op("pool", lambda e, s=s: e.memset(va_h[:, s, 64:192], 1.0), writes=[va_ring[s].b])

        ang, kf, rr, rc = ropescr(0), ropescr(1), ropescr(2), ropescr(3)
        ki_ap = kf.ap.bitcast(I32)
        wb = [w_out.b]
        P.op("dve", lambda e: e.tensor_copy(out=posf.ap, in_=posi.ap), reads=[posi.b], writes=[posf.b])
        P.op("dve", lambda e: e.tensor_tensor(
            out=ang.ap.rearrange("p (a b) -> p a b", a=NB),
            in0=posf.ap.unsqueeze(2).to_broadcast([128, NB, 32]),
            in1=invf_bc.ap.unsqueeze(1).to_broadcast([128, NB, 32]), op=ALU.mult),
            reads=[posf.b, invf_bc.b], writes=wb)
        P.op("dve", lambda e: e.tensor_scalar(out=rr.ap, in0=ang.ap, scalar1=1.0 / TWO_PI, scalar2=None, op0=ALU.mult),
             reads=wb, writes=wb)
        P.op("dve", lambda e: e.tensor_copy(out=ki_ap, in_=rr.ap), reads=wb, writes=wb)
        P.op("dve", lambda e: e.tensor_copy(out=rc.ap, in_=ki_ap), reads=wb, writes=wb)
        P.op("dve", lambda e: e.scalar_tensor_tensor(out=rr.ap, in0=rc.ap, scalar=-CW1, in1=ang.ap,
                                                     op0=ALU.mult, op1=ALU.add), reads=wb, writes=wb)
        P.op("dve", lambda e: e.scalar_tensor_tensor(out=rr.ap, in0=rc.ap, scalar=-CW2, in1=rr.ap,
                                                     op0=ALU.mult, op1=ALU.add), reads=wb, writes=wb)

        def wrap(t, tmp):
            P.op("dve", lambda e: e.tensor_scalar(out=tmp.ap, in0=t.ap, scalar1=math.pi, scalar2=-TWO_PI,
                                                  op0=ALU.is_gt, op1=ALU.mult), reads=wb, writes=wb)
            P.op("dve", lambda e: e.tensor_tensor(out=t.ap, in0=t.ap, in1=tmp.ap, op=ALU.add), reads=wb, writes=wb)
            P.op("dve", lambda e: e.tensor_scalar(out=tmp.ap, in0=t.ap, scalar1=-math.pi, scalar2=TWO_PI,
                                                  op0=ALU.is_lt, op1=ALU.mult), reads=wb, writes=wb)
            P.op("dve", lambda e: e.tensor_tensor(out=t.ap, in0=t.ap, in1=tmp.ap, op=ALU.add), reads=wb, writes=wb)
            P.op("dve", lambda e: e.tensor_scalar(out=t.ap, in0=t.ap, scalar1=-math.pi, scalar2=math.pi,
                                                  op0=ALU.max, op1=ALU.min), reads=wb, writes=wb)

        wrap(rr, rc)
        P.op("dve", lambda e: e.tensor_scalar(out=ang.ap, in0=rr.ap, scalar1=math.pi / 2, scalar2=None, op0=ALU.add),
             reads=wb, writes=wb)
        wrap(ang, rc)
        P.op("act", lambda e: e.activation(out=sin_h[:, :, :].rearrange("p a b -> p (a b)"), in_=rr.ap, func=AF.Sin),
             reads=wb, writes=[sinT.b])
        P.op("act", lambda e: e.activation(out=cos_h[:, :, :].rearrange("p a b -> p (a b)"), in_=ang.ap, func=AF.Sin),
             reads=wb, writes=[cosT.b])

        ada_banks = [next_bank() for _ in range(6)]
        for k in range(8):
            for nb in range(6):
                P.op("pe", lambda e, k=k, nb=nb: e.matmul(
                    ada_banks[nb].ap, crep3[:, k, :], wada[k].ap[:, nb * 512:(nb + 1) * 512],
                    start=(k == 0), stop=(k == 7)),
                    reads=[crep.b, wada[k].b], writes=[ada_banks[nb].b])
        bada_loads = {}
        bada2 = T(w_pb_h[:, :, :].rearrange("p a b -> p (a b)").bitcast(F32)[:, 1024:2048], "bada2", buf=w_pb.b)

        def load_bada(t3):
            bb = bada_bcs[t3 % 2]
            bada_loads[t3] = dma("sp", bb, bb.ap, bada_d[t3 * 1024:(t3 + 1) * 1024].partition_broadcast(128), sem_bada[t3 % 2])

        for t3 in (1, 0, 2):
            bada_bc = bada_bcs[t3 % 2] if t3 < 2 else bada2
            for h in range(2):
                bk = ada_banks[t3 * 2 + h]
                sl = slice(h * 512, (h + 1) * 512)
                if t3 == 0:
                    P.op("dve", lambda e, bk=bk, sl=sl, bb=bada_bc: e.tensor_tensor(out=sh_h[:, sl], in0=bk.ap, in1=bb.ap[:, sl],
                                                                          op=ALU.add),
                         reads=[bk.b, bada_bc.b], writes=[shift_bc.b])
                elif t3 == 1:
                    P.op("dve", lambda e, bk=bk, sl=sl, bb=bada_bc: e.scalar_tensor_tensor(
                        out=gs_h[:, sl], in0=bk.ap, scalar=1.0, in1=bb.ap[:, sl], op0=ALU.add, op1=ALU.add),
                        reads=[bk.b, bada_bc.b], writes=[gs_bc.b])
                    P.op("dve", lambda e, sl=sl: e.tensor_tensor(out=gs_h[:, sl], in0=gs_h[:, sl], in1=gpre_bc.ap[:, sl],
                                                                  op=ALU.mult),
                         reads=[gs_bc.b, gpre_bc.b], writes=[gs_bc.b])
                else:
                    P.op("dve", lambda e, bk=bk, sl=sl, bb=bada_bc: e.tensor_tensor(out=gg_h[:, sl], in0=bk.ap, in1=bb.ap[:, sl],
                                                                          op=ALU.add),
                         reads=[bk.b, bada_bc.b], writes=[gg_bc.b])
                    P.op("dve", lambda e, sl=sl: e.tensor_tensor(out=gg_h[:, sl], in0=gg_h[:, sl], in1=gpost_bc.ap[:, sl],
                                                                  op=ALU.mult),
                         reads=[gg_bc.b, gpost_bc.b], writes=[gg_bc.b])

        wsf, trf = ropescr(2), ropescr(3)
        o1_ = dma("sp", wsf, wsf.ap[:, 0:512], wst_d.rearrange("p a b -> p (a b)"), sem_bc)
        o2_ = dma("sp", trf, trf.ap[:, 0:512], triu_d.rearrange("p a b -> p (a b)"), sem_bc)
        P.op("dve", lambda e: e.tensor_tensor(out=wT_h[:, :, :].rearrange("p a b -> p (a b)"),
                                              in0=wsf.ap[:, 0:512], in1=trf.ap[:, 0:512], op=ALU.mult),
             reads=wb, writes=[wT.b])

        def load_remaining_weights():
            load_win_group(3)
            load_win_group(4)
            P.op("pool", lambda e: e.dma_start(out=w_pa_h[:, :, :], in_=wpa_d.rearrange("(k p) n -> p k n", p=128)),
                 writes=[w_pa.b], dma_sem=sem_wp[0])
            P.op("pool", lambda e: e.dma_start(out=w_pb_h[:, :, :], in_=wpb_d.rearrange("(k p) n -> p k n", p=128)),
                 writes=[w_pb.b], dma_sem=sem_wp[1])
            P.op("pool", lambda e: e.dma_start(out=w_out_h[:, :, :], in_=wout_d.rearrange("(k p) n -> p k n", p=128)),
                 writes=[w_out.b], dma_sem=sem_wp[2])
            w_out.b.ro = True
            w_pa.b.ro = True
            w_pb.b.ro = True


        alias_rs = {}
        alias_ws = []
        for t in setup_tiles:
            for k_, r in t.b.rs.items():
                alias_rs[(k_, id(t))] = r
            if t.b.w is not None:
                alias_ws.append(t.b.w)

        moff = [0]

        def mt(nwords, dt, name):
            nwords = (nwords + 7) // 8 * 8
            ap = scr_h[:, moff[0]:moff[0] + nwords]
            moff[0] += nwords
            assert moff[0] <= SCR_WORDS, (name, moff[0])
            if dt is BF16:
                ap = ap.bitcast(BF16)
            t = T(ap, name)
            i = 0
            for r in list(alias_rs.values()) + alias_ws:
                t.b.rs[("alias", i)] = r
                i += 1
            return t

        x_in = mt(1024, F32, "x_in")
        x_in_main = x_in
        hp = mt(1024, F32, "hp")
        h_bf = [mt(512, BF16, "h_bf%d" % i) for i in range(2)]
        hT = [mt(512, BF16, "hT%d" % i) for i in range(2)]
        tA = mt(512, F32, "tA")
        tB = mt(512, F32, "tB")
        tAk = mt(128, F32, "tAk")
        tBk = mt(128, F32, "tBk")
        qk_r = mt(320, BF16, "qk_r")
        qT = [mt(256, BF16, "qT%d" % i) for i in range(2)]
        sza = mt(256, BF16, "sza")
        szaT = [mt(256, BF16, "szaT%d" % i) for i in range(2)]
        Et = [mt(256, BF16, "E%d" % i) for i in range(4)]
        Pm = [mt(256, BF16, "Pm%d" % i) for i in range(4)]
        den = mt(512, F32, "den")
        ytmp = mt(512, F32, "ytmp")
        yaT = mt(256, BF16, "yaT")
        gv = mt(512, F32, "gv")
        vn_bf = [mt(256, BF16, "vn_bf%d" % i) for i in range(2)]
        su = [mt(256, BF16, "su%d" % i) for i in range(2)]
        szb = mt(256, BF16, "szb")
        sgz = mt(256, BF16, "sgz")
        sgz2 = mt(256, BF16, "sgz2")
        yb = mt(256, BF16, "yb")
        ybT = mt(256, BF16, "ybT")
        sig = mt(1024, BF16, "sig")
        m1 = mt(512, F32, "m1")
        m2 = mt(512, F32, "m2")
        merged = mt(512, BF16, "merged")
        mTt = mt(512, BF16, "mT")
        outt = mt(1024, F32, "out")

        w_in_ap = w_in_h

        def mm_group(outs, lhs_fn, K, reads_fn, mid=None):
            for k in range(K):
                if mid is not None and k == K // 2:
                    mid()
                for (bk, width, rhs_fn, extra) in outs:
                    P.op("pe", lambda e, k=k, bk=bk, width=width, rhs_fn=rhs_fn: e.matmul(
                        bk.ap[:, 0:width], lhs_fn(k), rhs_fn(k), start=(k == 0), stop=(k == K - 1)),
                        reads=reads_fn(k) + extra, writes=[bk.b])

        def XLOAD(n):
            P.tag = "XL(%d)" % n
            if n < 2:
                return
            dma("act", x_in, x_in.ap, x_d[n * 128:(n + 1) * 128, :], sem_x)

        def S0a(n):
            P.tag = "S0a(%d)" % n
            x_in = x_in_main if n >= 2 else x_stage[n]
            P.op("act", lambda e: e.activation(out=hp.ap, in_=x_in.ap, func=AF.Square, accum_out=ssq.ap),
                 reads=[x_in.b], writes=[hp.b, ssq.b])
            P.op("dve", lambda e: e.tensor_scalar(out=rs1.ap, in0=ssq.ap, scalar1=1.0 / D, scalar2=EPS,
                                                  op0=ALU.mult, op1=ALU.add), reads=[ssq.b], writes=[rs1.b])
            P.op("pool", lambda e: e.tensor_tensor(out=rs1.ap, in0=rs1.ap, in1=nhalf.ap, op=ALU.pow),
                 reads=[rs1.b, nhalf.b], writes=[rs1.b])

        def S0b(n):
            P.tag = "S0b(%d)" % n
            hb = h_bf[n % 2]
            x_in = x_in_main if n >= 2 else x_stage[n]
            P.op("dve", lambda e: e.scalar_tensor_tensor(out=hp.ap, in0=x_in.ap, scalar=rs1.ap, in1=gs_bc.ap,
                                                         op0=ALU.mult, op1=ALU.mult),
                 reads=[x_in.b, rs1.b, gs_bc.b], writes=[hp.b])
            P.op("dve", lambda e: e.tensor_tensor(out=hb.ap, in0=hp.ap, in1=shift_bc.ap, op=ALU.add),
                 reads=[hp.b, shift_bc.b], writes=[hb.b])

        def S0(n):
            S0a(n)
            S0b(n)

        def F1(n):
            P.tag = "F1(%d)" % n
            hb, ht = h_bf[n % 2], hT[n % 2]
            bk = next_bank()
            pb = bk.ap.bitcast(BF16)
            for k in range(8):
                P.op("pe", lambda e, k=k: e.transpose(pb[:, k * 128:(k + 1) * 128], hb.ap[:, k * 128:(k + 1) * 128], ident.ap),
                     reads=[hb.b, ident.b], writes=[bk.b])
            P.op("act", lambda e: e.activation(out=ht.ap, in_=pb, func=AF.Copy), reads=[bk.b], writes=[ht.b])

        def proj_banks(n, specs, mid=None):
            ht = hT[n % 2]
            ht3 = ht.ap.rearrange("p (k m) -> p k m", k=8)
            outs = []
            for (c0, width) in specs:
                bk = next_bank()
                outs.append((bk, width, (lambda k, c0=c0, width=width: w_in_ap[:, k, c0:c0 + width]), [wgrp(c0).b]))
            mm_group(outs, lambda k: ht3[:, k, :], 8, lambda k: [ht.b], mid=mid)
            return [o[0] for o in outs]

        def rope(src_bank, width, nh, dst_off, n, tA, tB):
            src = src_bank.ap[:, 0:width]
            s4 = src.rearrange("p (h t f) -> p h t f", h=nh, t=2, f=32)
            s3 = src.rearrange("p (h f) -> p h f", h=2 * nh, f=32)
            a3 = tA.ap[:, 0:width].rearrange("p (h f) -> p h f", h=2 * nh, f=32)
            b4 = tB.ap[:, 0:width].rearrange("p (h t f) -> p h t f", h=nh, t=2, f=32)
            cs = cos_h[:, n, :].unsqueeze(1).to_broadcast([128, 2 * nh, 32])
            sn = sin_h[:, n, :].unsqueeze(1).to_broadcast([128, nh, 32])
            P.op("dve", lambda e: e.tensor_tensor(out=a3, in0=s3, in1=cs, op=ALU.mult),
                 reads=[src_bank.b, cosT.b], writes=[tA.b])
            P.op("dve", lambda e: e.scalar_tensor_tensor(out=b4[:, :, 0, :], in0=s4[:, :, 1, :], scalar=-1.0, in1=sn,
                                                         op0=ALU.mult, op1=ALU.mult),
                 reads=[src_bank.b, sinT.b], writes=[tB.b])
            P.op("dve", lambda e: e.tensor_tensor(out=b4[:, :, 1, :], in0=s4[:, :, 0, :], in1=sn, op=ALU.mult),
                 reads=[src_bank.b, sinT.b], writes=[tB.b])
            P.op("pool", lambda e: e.tensor_tensor(out=qk_r.ap[:, dst_off:dst_off + width],
                                                   in0=tA.ap[:, 0:width],
                                                   in1=tB.ap[:, 0:width], op=ALU.add),
                 reads=[tA.b, tB.b], writes=[qk_r.b])

        za_bank = {}
        qk_bank = {}

        def FR1(n, mid=None):
            P.tag = "FR1(%d)" % n
            tag_ = P.tag

            def mid_():
                if mid is not None:
                    mid()
                    P.tag = tag_

            bq, bkv, bza = proj_banks(n, [(C_Q, 512), (C_K, 256), (C_ZA, 512)], mid=mid_)
            va = va_ring[n % 3]
            va4 = va.ap.rearrange("p (a b) -> p a b", a=4)
            P.op("act", lambda e: e.activation(out=va4[:, 0:4:3, :], in_=bkv.ap[:, 128:256].rearrange("p (a b) -> p a b", a=2),
                                               func=AF.Copy), reads=[bkv.b], writes=[va.b])
            za_bank[n] = bza
            qk_bank[n] = (bq, bkv)
            pin(bza, bq, bkv)

        def FR1rope(n):
            P.tag = "FR1rope(%d)" % n
            bq, bkv = qk_bank.pop(n)
            a3 = tA.ap[:, 0:512].rearrange("p (h f) -> p h f", h=16, f=32)
            a4 = tA.ap[:, 0:512].rearrange("p (h t f) -> p h t f", h=8, t=2, f=32)
            b4 = tB.ap[:, 0:512].rearrange("p (h t f) -> p h t f", h=8, t=2, f=32)
            o4 = qk_r.ap[:, 0:512].rearrange("p (h t f) -> p h t f", h=8, t=2, f=32)
            cs = cos_h[:, n, :].unsqueeze(1).to_broadcast([128, 16, 32])
            sn = sin_h[:, n, :].unsqueeze(1).to_broadcast([128, 8, 32])
            P.op("dve", lambda e: e.tensor_copy(out=tA.ap[:, 0:512], in_=bq.ap[:, 0:512]), reads=[bq.b], writes=[tA.b])
            rope(bkv, 128, 2, 512, n, tAk, tBk)
            P.op("pool", lambda e: e.tensor_tensor(out=b4[:, :, 0, :], in0=a4[:, :, 1, :], in1=sn, op=ALU.mult),
                 reads=[tA.b, sinT.b], writes=[tB.b])
            P.op("pool", lambda e: e.tensor_tensor(out=b4[:, :, 1, :], in0=a4[:, :, 0, :], in1=sn, op=ALU.mult),
                 reads=[tA.b, sinT.b], writes=[tB.b])
            P.op("pool", lambda e: e.tensor_tensor(out=a3, in0=a3, in1=cs, op=ALU.mult),
                 reads=[tA.b, cosT.b], writes=[tA.b])
            P.op("pool", lambda e: e.tensor_tensor(out=o4[:, :, 0, :], in0=a4[:, :, 0, :], in1=b4[:, :, 0, :], op=ALU.subtract),
                 reads=[tA.b, tB.b], writes=[qk_r.b])
            P.op("pool", lambda e: e.tensor_tensor(out=o4[:, :, 1, :], in0=a4[:, :, 1, :], in1=b4[:, :, 1, :], op=ALU.add),
                 reads=[tA.b, tB.b], writes=[qk_r.b])
            unpin(bq, bkv)

        def FR1b(n):
            P.tag = "FR1b(%d)" % n
            bza = za_bank.pop(n)
            P.op("act", lambda e: e.activation(out=sgz.ap, in_=bza.ap, func=AF.Sigmoid), reads=[bza.b], writes=[sgz.b])
            P.op("dve", lambda e: e.tensor_tensor(out=sza.ap, in0=bza.ap, in1=sgz.ap, op=ALU.mult),
                 reads=[bza.b, sgz.b], writes=[sza.b])
            unpin(bza)

        fr2_banks = {}

        def FR2(n):
            P.tag = "FR2(%d)" % n
            sl = n % 2
            fr2_banks[n] = proj_banks(n, [(C_ZB, 512), (C_VB, 512), (C_U, 512)])
            pin(*fr2_banks[n])

        def FR2act(n):
            P.tag = "FR2act(%d)" % n
            sl = n % 2
            bzb, bvb, bu = fr2_banks[n]
            P.op("act", lambda e: e.activation(out=sgz2.ap, in_=bzb.ap, func=AF.Sigmoid), reads=[bzb.b], writes=[sgz2.b])
            P.op("act", lambda e: e.activation(out=gv.ap, in_=bvb.ap, func=AF.Gelu), reads=[bvb.b], writes=[gv.b])
            P.op("act", lambda e: e.activation(out=su[sl].ap, in_=bu.ap, func=AF.Gelu), reads=[bu.b], writes=[su[sl].b])

        def FR2ev(n):
            P.tag = "FR2ev(%d)" % n
            sl = n % 2
            bzb, bvb, bu = fr2_banks.pop(n)
            unpin(bzb, bvb, bu)
            P.op("dve", lambda e: e.tensor_tensor(out=szb.ap, in0=bzb.ap, in1=sgz2.ap, op=ALU.mult),
                 reads=[bzb.b, sgz2.b], writes=[szb.b])
            P.op("dve", lambda e: e.bn_stats(out=bnst.ap, in_=gv.ap), reads=[gv.b], writes=[bnst.b])
            P.op("dve", lambda e: e.bn_aggr(out=bnmv.ap, in_=bnst.ap), reads=[bnst.b], writes=[bnmv.b])
            P.op("dve", lambda e: e.tensor_scalar(out=rs2.ap, in0=bnmv.ap[:, 1:2], scalar1=EPS, scalar2=None, op0=ALU.add),
                 reads=[bnmv.b], writes=[rs2.b])
            P.op("pool", lambda e: e.tensor_tensor(out=rs2.ap, in0=rs2.ap, in1=nhalf.ap, op=ALU.pow),
                 reads=[rs2.b, nhalf.b], writes=[rs2.b])
            P.op("dve", lambda e: e.tensor_scalar(out=gv.ap, in0=gv.ap, scalar1=bnmv.ap[:, 0:1], scalar2=rs2.ap,
                                                  op0=ALU.subtract, op1=ALU.mult),
                 reads=[gv.b, bnmv.b, rs2.b], writes=[gv.b])
            P.op("pool", lambda e: e.tensor_tensor(out=gv.ap, in0=gv.ap, in1=lg_bc.ap, op=ALU.mult),
                 reads=[gv.b, lg_bc.b], writes=[gv.b])
            P.op("pool", lambda e: e.tensor_tensor(out=vn_bf[sl].ap, in0=gv.ap, in1=lb_bc.ap, op=ALU.add),
                 reads=[gv.b, lb_bc.b], writes=[vn_bf[sl].b])
            P.op("pool", lambda e: e.tensor_tensor(out=su[sl].ap, in0=su[sl].ap, in1=szb.ap, op=ALU.mult),
                 reads=[su[sl].b, szb.b], writes=[su[sl].b])

        def FR3(n):
            P.tag = "FR3(%d)" % n
            sl = n % 2
            bt = next_bank()
            pbt = bt.ap.bitcast(BF16)
            for j in range(4):
                P.op("pe", lambda e, j=j: e.transpose(pbt[:, j * 128:(j + 1) * 128], qk_r.ap[:, j * 128:(j + 1) * 128], ident.ap),
                     reads=[qk_r.b, ident.b], writes=[bt.b])
            for j in range(4):
                P.op("pe", lambda e, j=j: e.transpose(pbt[:, 512 + j * 128:512 + (j + 1) * 128],
                                                      sza.ap[:, j * 128:(j + 1) * 128], ident.ap),
                     reads=[sza.b, ident.b], writes=[bt.b])
            P.op("dve", lambda e: e.tensor_copy(out=qT[sl].ap, in_=pbt[:, 0:512]), reads=[bt.b], writes=[qT[sl].b])
            P.op("act", lambda e: e.activation(out=szaT[sl].ap, in_=pbt[:, 512:1024], func=AF.Copy),
                 reads=[bt.b], writes=[szaT[sl].b])

        def KT(n, bt=None, col0=0):
            if bt is None:
                bt = next_bank()
            pbt = bt.ap.bitcast(BF16)
            P.op("pe", lambda e: e.transpose(pbt[:, col0:col0 + 128], qk_r.ap[:, 512:640], ident.ap),
                 reads=[qk_r.b, ident.b], writes=[bt.b])
            P.op("dve", lambda e: e.tensor_copy(out=kT_ring[n % 3].ap, in_=pbt[:, col0:col0 + 128]), reads=[bt.b],
                 writes=[kT_ring[n % 3].b])

        def SP(n):
            P.tag = "SP(%d)" % n
            sl = n % 2
            bsv = next_bank()
            for g in range(4):
                P.op("pe", lambda e, g=g: e.matmul(bsv.ap[:, g * 128:(g + 1) * 128], wT_h[:, g, :],
                                                   vn_bf[sl].ap[:, g * 128:(g + 1) * 128], start=True, stop=True),
                     reads=[wT.b, vn_bf[sl].b], writes=[bsv.b])
            for g in range(4):
                P.op("dve", lambda e, g=g: e.scalar_tensor_tensor(
                    out=yb.ap[:, g * 128:(g + 1) * 128], in0=bsv.ap[:, g * 128:(g + 1) * 128], scalar=bs_h[:, g:g + 1],
                    in1=su[sl].ap[:, g * 128:(g + 1) * 128], op0=ALU.add, op1=ALU.mult),
                    reads=[bsv.b, bs_col.b, su[sl].b], writes=[yb.b])

        sc_state = {}

        def SC(n, kvhs=(0, 1)):
            P.tag = "SC(%d)" % n
            sl = n % 2
            kbs = ([] if n == 0 else [(n - 1, 0)]) + [(n, 1)]
            q3 = qT[sl].ap
            st = sc_state.setdefault(n, {"pm": {}, "ei": 0})
            for kvh in kvhs:
                ps_ = slice(kvh * 64, (kvh + 1) * 64)
                for (kb, mi) in kbs:
                    bs_ = next_bank()
                    kt = kT_ring[kb % 3]
                    P.op("pe", lambda e, bs_=bs_, kt=kt, ps_=ps_: e.matmul(bs_.ap, kt.ap[ps_, :], q3[ps_, :], start=True, stop=True),
                         reads=[kt.b, qT[sl].b], writes=[bs_.b])
                    et = Et[st["ei"] % 4]
                    pm = Pm[kvh * 2 + mi]
                    st["ei"] += 1
                    P.op("act", lambda e, bs_=bs_, et=et: e.activation(out=et.ap, in_=bs_.ap, func=AF.Exp, scale=0.125),
                         reads=[bs_.b], writes=[et.b])
                    P.op("dve", lambda e, et=et, pm=pm, mi=mi: e.tensor_tensor(
                        out=pm.ap.rearrange("p (g q) -> p g q", g=4), in0=et.ap.rearrange("p (g q) -> p g q", g=4),
                        in1=masks_h[:, mi, :].unsqueeze(1).to_broadcast([128, 4, 128]), op=ALU.mult),
                         reads=[et.b, masks.b], writes=[pm.b])
                    st["pm"][(kvh, mi)] = (pm, kb)
            return kbs, st["pm"]

        gt_banks = {}

        def GT(n):
            P.tag = "GT(%d)" % n
            gt_banks[n] = proj_banks(n, [(C_GA, 512), (C_GA + 512, 512), (C_GB, 512), (C_GB + 512, 512)])
            pin(*gt_banks[n])

        def GTact(n, preload_exp=False):
            P.tag = "GTact(%d)" % n
            gbs_ = gt_banks.pop(n)
            unpin(*gbs_)
            for i, bk in enumerate(gbs_):
                off = i * 512
                P.op("act", lambda e, bk=bk, off=off: e.activation(out=sig.ap[:, off:off + 512], in_=bk.ap, func=AF.Sigmoid),
                     reads=[bk.b], writes=[sig.b])
            if preload_exp:
                P.op("act", lambda e: e.activation(out=gjunk.ap, in_=gjunk.ap, func=AF.Exp), reads=[], writes=[gjunk.b])

        pv_banks = {}

        def PV(n, kbs, pm_list):
            P.tag = "PV(%d)" % n
            sl = n % 2
            po = []
            for kvh in range(2):
                bo = next_bank()
                for i, (kb, mi) in enumerate(kbs):
                    pm, _ = pm_list[(kvh, mi)]
                    va = va_ring[kb % 3]
                    P.op("pe", lambda e, bo=bo, va=va, pm=pm, i=i, kvh=kvh: e.matmul(
                        bo.ap, va.ap[:, kvh * 128:(kvh + 1) * 128], pm.ap, start=(i == 0), stop=(i == len(kbs) - 1)),
                        reads=[va.b, pm.b], writes=[bo.b])
                po.append(bo)
            esk2 = esk_h[:, :, :].rearrange("p a b -> p (a b)")
            for kvh in range(2):
                s_rows = slice(64, 128) if kvh == 0 else slice(0, 64)
                bo = po[kvh]
                P.op("dve", lambda e, bo=bo, s_rows=s_rows: e.tensor_tensor(out=den.ap[s_rows, :], in0=bo.ap[s_rows, :],
                                                                              in1=esk2[s_rows, :], op=ALU.add),
                     reads=[bo.b, esink_t.b], writes=[den.b])
            P.op("act", lambda e: e.activation(out=den.ap, in_=den.ap, func=AF.Ln), reads=[den.b], writes=[den.b])
            P.op("act", lambda e: e.activation(out=den.ap, in_=den.ap, func=AF.Exp, scale=-1.0), reads=[den.b], writes=[den.b])
            pv_banks[n] = po
            pin(*po)

        def PVb(n):
            P.tag = "PVb(%d)" % n
            sl = n % 2
            po = pv_banks.pop(n)
            unpin(*po)
            for kvh in range(2):
                o_rows = slice(0, 64) if kvh == 0 else slice(64, 128)
                s_rows = slice(64, 128) if kvh == 0 else slice(0, 64)
                bo = po[kvh]
                P.op("dve", lambda e, bo=bo, s_rows=s_rows, o_rows=o_rows: e.tensor_tensor(
                    out=ytmp.ap[o_rows, :], in0=bo.ap[o_rows, :], in1=den.ap[s_rows, :], op=ALU.mult),
                    reads=[bo.b, den.b], writes=[ytmp.b])
            P.op("pool", lambda e: e.tensor_tensor(out=yaT.ap, in0=ytmp.ap, in1=szaT[sl].ap, op=ALU.mult),
                 reads=[ytmp.b, szaT[sl].b], writes=[yaT.b])

        def YBT(n, with_k_of=None):
            P.tag = "YBT(%d)" % n
            bt = next_bank()
            pbt = bt.ap.bitcast(BF16)
            for j in range(4):
                P.op("pe", lambda e, j=j: e.transpose(pbt[:, j * 128:(j + 1) * 128], yb.ap[:, j * 128:(j + 1) * 128], ident.ap),
                     reads=[yb.b, ident.b], writes=[bt.b])
            if with_k_of is not None:
                P.op("pe", lambda e: e.transpose(pbt[:, 512:640], qk_r.ap[:, 512:640], ident.ap),
                     reads=[qk_r.b, ident.b], writes=[bt.b])
            P.op("dve", lambda e: e.tensor_copy(out=ybT.ap, in_=pbt[:, 0:512]), reads=[bt.b], writes=[ybT.b])
            if with_k_of is not None:
                kt = kT_ring[with_k_of % 3]
                P.op("dve", lambda e: e.tensor_copy(out=kt.ap, in_=pbt[:, 512:640]), reads=[bt.b], writes=[kt.b])

        def PAB(n):
            P.tag = "PAB(%d)" % n
            ya3 = yaT.ap.rearrange("p (g m) -> p g m", g=4)
            yb3 = ybT.ap.rearrange("p (g m) -> p g m", g=4)
            pab = [next_bank(), next_bank()]
            mm_group([(pab[h], 512, (lambda k, h=h: w_pa_h[:, k, h * 512:(h + 1) * 512]), [w_pa.b]) for h in range(2)],
                     lambda k: ya3[:, k, :], 4, lambda k: [yaT.b])
            pbb = [next_bank(), next_bank()]
            mm_group([(pbb[h], 512, (lambda k, h=h: w_pb_h[:, k, h * 512:(h + 1) * 512]), [w_pb.b]) for h in range(2)],
                     lambda k: yb3[:, k, :], 4, lambda k: [ybT.b])
            for h in range(2):
                sl_ = slice(h * 512, (h + 1) * 512)
                mm = m1 if h == 0 else m2
                P.op("dve", lambda e, h=h, mm=mm, sl_=sl_: e.tensor_tensor(out=mm.ap, in0=pab[h].ap,
                                                                  in1=sig.ap[:, sl_], op=ALU.mult),
                     reads=[pab[h].b, sig.b], writes=[mm.b])
            for h in range(2):
                sl_ = slice(h * 512, (h + 1) * 512)
                mm = m1 if h == 0 else m2
                tmp_ = ytmp if h == 0 else den
                P.op("dve", lambda e, h=h, sl_=sl_, tmp_=tmp_: e.tensor_tensor(out=tmp_.ap, in0=pbb[h].ap,
                                                                    in1=sig.ap[:, 1024 + h * 512:1536 + h * 512], op=ALU.mult),
                     reads=[pbb[h].b, sig.b], writes=[tmp_.b])
                P.op("pool", lambda e, sl_=sl_, mm=mm, tmp_=tmp_: e.tensor_tensor(out=merged.ap[:, sl_], in0=tmp_.ap, in1=mm.ap,
                                                                               op=ALU.add),
                     reads=[tmp_.b, mm.b], writes=[merged.b])

        def MT(n):
            P.tag = "MT(%d)" % n
            dma("act", outt, outt.ap, x_d[n * 128:(n + 1) * 128, :], sem_o)
            bt = next_bank()
            pbt = bt.ap.bitcast(BF16)
            for j in range(8):
                P.op("pe", lambda e, j=j: e.transpose(pbt[:, j * 128:(j + 1) * 128], merged.ap[:, j * 128:(j + 1) * 128], ident.ap),
                     reads=[merged.b, ident.b], writes=[bt.b])
            P.op("dve", lambda e: e.tensor_copy(out=mTt.ap, in_=pbt), reads=[bt.b], writes=[mTt.b])

        def WO(n):
            P.tag = "WO(%d)" % n
            m3 = mTt.ap.rearrange("p (k m) -> p k m", k=8)
            pyb = [next_bank(), next_bank()]
            mm_group([(pyb[h], 512, (lambda k, h=h: w_out_h[:, k, h * 512:(h + 1) * 512]), [w_out.b]) for h in range(2)],
                     lambda k: m3[:, k, :], 8, lambda k: [mTt.b])
            for h in range(2):
                jk = sgz if h == 0 else sgz2
                P.op("act", lambda e, h=h, jk=jk: e.activation(out=jk.ap, in_=pyb[h].ap, func=AF.Square,
                                                               accum_out=ssq2.ap[:, h:h + 1]),
                     reads=[pyb[h].b], writes=[jk.b, ssq2.b])
            for h in range(2):
                sl_ = slice(h * 512, (h + 1) * 512)
                mm = m1 if h == 0 else m2
                P.op("dve", lambda e, h=h, sl_=sl_, mm=mm: e.tensor_tensor(out=mm.ap, in0=pyb[h].ap, in1=gg_h[:, sl_], op=ALU.mult),
                     reads=[pyb[h].b, gg_bc.b], writes=[mm.b])
            P.op("dve", lambda e: e.tensor_tensor(out=rs3.ap, in0=ssq2.ap[:, 0:1], in1=ssq2.ap[:, 1:2], op=ALU.add),
                 reads=[ssq2.b], writes=[rs3.b])

        def WOfin(n, split_store=False):
            P.tag = "WOfin(%d)" % n
            P.op("act", lambda e: e.activation(out=rs3.ap, in_=rs3.ap, func=AF.Ln, scale=1.0 / D, bias=eps_col.ap),
                 reads=[rs3.b, eps_col.b], writes=[rs3.b])
            P.op("act", lambda e: e.activation(out=rs3.ap, in_=rs3.ap, func=AF.Exp, scale=-0.5),
                 reads=[rs3.b], writes=[rs3.b])
            for h in range(2):
                sl_ = slice(h * 512, (h + 1) * 512)
                mm = m1 if h == 0 else m2
                P.op("dve", lambda e, sl_=sl_, mm=mm: e.scalar_tensor_tensor(
                    out=outt.ap[:, sl_], in0=mm.ap, scalar=rs3.ap, in1=outt.ap[:, sl_], op0=ALU.mult, op1=ALU.add),
                    reads=[mm.b, rs3.b, outt.b], writes=[outt.b])
                if split_store:
                    P.op("sp", lambda e, sl_=sl_: e.dma_start(out=y_d[n * 128:(n + 1) * 128, sl_], in_=outt.ap[:, sl_]),
                         reads=[outt.b], writes=[], dma_sem=sem_o)
            if split_store:
                return None
            return P.op("sp", lambda e: e.dma_start(out=y_d[n * 128:(n + 1) * 128, :], in_=outt.ap),
                        reads=[outt.b], writes=[], dma_sem=sem_o)

        def WARM(count):
            P.tag = "WARM"
            bk = next_bank()
            for _ in range(count):
                P.op("pe", lambda e: e.matmul(bk.ap, ident.ap, w_in_h[:, 0, 0:512], start=True, stop=True),
                     reads=[ident.b, w_in_grp[0].b], writes=[bk.b])

        XLOAD(0)
        S0(0)
        if nblk > 1:
            XLOAD(1)
            S0(1)
        load_remaining_weights()
        F1(0)
        FR1(0)
        FR1rope(0)
        FR1b(0)
        FR2(0)
        FR2act(0)
        P.tag = "KT(0)"
        KT(0)
        FR3(0)
        if nblk > 1:
            F1(1)
        FR2ev(0)
        for n in range(nblk):
            nx = n + 1 < nblk
            if n + 2 < nblk:
                XLOAD(n + 2)
            if nx:
                kbs, pml = SC(n, kvhs=(0,))
                FR1(n + 1, mid=lambda n=n: SC(n, kvhs=(1,)))
            else:
                kbs, pml = SC(n)
                if nblk > 1:
                    WARM(16)
            if nx:
                FR1rope(n + 1)
            PV(n, kbs, pml)
            PVb(n)
            SP(n)
            if n > 0:
                WOfin(n - 1)
            if nx:
                FR1b(n + 1)
            if n + 2 < nblk:
                S0a(n + 2)
            GT(n)
            YBT(n, with_k_of=(n + 1 if nx else None))
            GTact(n, preload_exp=(not nx and nblk > 1))
            PAB(n)
            if not nx and nblk > 1:
                WARM(18)
            if n + 2 < nblk:
                S0b(n + 2)
            if nx:
                FR2(n + 1)
                FR2act(n + 1)
            MT(n)
            if n + 2 < nblk:
                F1(n + 2)
            if nx:
                FR3(n + 1)
                FR2ev(n + 1)
            WO(n)
        WOfin(nblk - 1)

        fin_val = sem_o.count

        P.finalize(eng_sems)
        global LAST_PROG
        LAST_PROG = P

        with nc.Block() as block:
            @block.tensor
            def _(e):
                P.emit("pe", e)

            @block.scalar
            def _(e):
                P.emit("act", e)

            @block.vector
            def _(e):
                P.emit("dve", e)

            @block.gpsimd
            def _(e):
                P.emit("pool", e)

            @block.sync
            def _(e):
                P.emit("sp", e)
                e.wait_ge(sem_o.handle, fin_val)
    return nc


def _col_perm():
    def headperm(base):
        cols = []
        for g in range(4):
            for kvh in range(2):
                h = kvh * 4 + g
                cols.extend(range(base + h * 64, base + (h + 1) * 64))
        return cols
    perm = []
    perm += headperm(0)
    perm += list(range(512, 640))
    perm += list(range(640, 768))
    perm += headperm(768)
    perm += list(range(2304, 2816))
    perm += list(range(1792, 2304))
    perm += list(range(1280, 1792))
    perm += list(range(2816, 3840))
    perm += list(range(3840, 4864))
    return np.asarray(perm, dtype=np.int64)


_NC_CACHE = {}
LAST_PROG = None


def kernel(x, c, positions, w_ada, b_ada, g_pre, g_post, w_in, sinks,
           ln_v_g, ln_v_b, w_s, b_s, w_proj_a, w_proj_b, w_out, _nblk=NB):
    f32 = np.float32
    x = np.asarray(x, f32)
    B = x.shape[0]
    perm = _col_perm()
    w_in_p = np.ascontiguousarray(np.asarray(w_in, f32)[0][:, perm])
    rows = []
    for g in range(4):
        for kvh in range(2):
            h = kvh * 4 + g
            rows.extend(range(h * 64, (h + 1) * 64))
    w_pa_p = np.ascontiguousarray(np.asarray(w_proj_a, f32)[0][rows, :])
    w_s_t = np.ascontiguousarray(np.transpose(np.asarray(w_s, f32)[0], (2, 0, 1)))
    b_s_col = np.ascontiguousarray(np.asarray(b_s, f32)[0].T)
    ident = np.eye(128, dtype=f32)
    kk = np.arange(128)[:, None]
    qq = np.arange(128)[None, :]
    mprev = (qq < kk).astype(f32)
    mcur = (kk <= qq).astype(f32)
    masks = np.stack([mprev, mcur], axis=1)
    triu = np.ascontiguousarray(np.broadcast_to((kk <= qq).astype(f32)[:, None, :], (128, 4, 128)))
    invf = (10000.0 ** (-np.arange(32, dtype=f32) / f32(32))).astype(f32)

    shared = {
        "w_ada": np.ascontiguousarray(np.asarray(w_ada, f32)[0]),
        "b_ada": np.ascontiguousarray(np.asarray(b_ada, f32)[0]),
        "g_pre": np.ascontiguousarray(np.asarray(g_pre, f32)[0]),
        "g_post": np.ascontiguousarray(np.asarray(g_post, f32)[0]),
        "w_in_p": w_in_p,
        "sinks": np.ascontiguousarray(np.asarray(sinks, f32)[0]),
        "ln_v_g": np.ascontiguousarray(np.asarray(ln_v_g, f32)[0]),
        "ln_v_b": np.ascontiguousarray(np.asarray(ln_v_b, f32)[0]),
        "w_s_t": w_s_t,
        "b_s_col": b_s_col,
        "w_pa_p": w_pa_p,
        "w_proj_b": np.ascontiguousarray(np.asarray(w_proj_b, f32)[0]),
        "w_out": np.ascontiguousarray(np.asarray(w_out, f32)[0]),
        "ident": ident,
        "masks": np.ascontiguousarray(masks),
        "triu": triu,
        "invf": invf,
    }
    c = np.asarray(c, f32)
    positions = np.asarray(positions, np.int32)
    in_maps = []
    for b in range(B):
        m = dict(shared)
        m["x"] = np.ascontiguousarray(x[b])
        m["c_col"] = np.ascontiguousarray(c[b].reshape(8, 128).T)
        m["pos_col"] = np.ascontiguousarray(positions[b].reshape(NB, 128).T)
        in_maps.append(m)
    if _nblk not in _NC_CACHE:
        _NC_CACHE[_nblk] = build_program(_nblk)
    nc = _NC_CACHE[_nblk]
    res = run_bass_kernel_spmd(nc, in_maps, core_ids=list(range(B)))
    out = np.stack([np.asarray(res.results[b]["y"], f32) for b in range(B)], axis=0)
    return out
```
